# Optimizing a Trainium2 kernel written in Bass

```python
import math
import jax, jax.numpy as jnp
from jax import lax
import numpy as np

D_MODEL = 2048
BATCH = 4
SEQ = 2048
DEPTH = 1

ATTN_WIDTH = D_MODEL // 2
REC_WIDTH = D_MODEL // 2
N_ATTN_HEADS = 8
ATTN_HEAD_DIM = ATTN_WIDTH // N_ATTN_HEADS
QK_DIM = ATTN_HEAD_DIM // 2
N_REC_BLOCKS = 8
REC_BLOCK = REC_WIDTH // N_REC_BLOCKS
CONV_WIDTH = 4
LRU_C = 8.0
ROPE_THETA = 10000.0
Q_BLOCK = 128
NORM_EPS = 1e-6
IN_WIDTH = 4 * ATTN_WIDTH + 2 * REC_WIDTH
SPLITS = [ATTN_WIDTH, 2 * ATTN_WIDTH, 3 * ATTN_WIDTH, 4 * ATTN_WIDTH, 4 * ATTN_WIDTH + REC_WIDTH]

kernel_name = "hymba_diffattn_rglru_hybrid"


def rms_norm(x, g):
    xf = x.astype(jnp.float32)
    y = xf * lax.rsqrt(jnp.mean(xf * xf, axis=-1, keepdims=True) + NORM_EPS)
    return (y * g.astype(jnp.float32)).astype(x.dtype)


def rope(t, cos, sin):
    tf = t.astype(jnp.float32)
    t1, t2 = jnp.split(tf, 2, axis=-1)
    out = jnp.concatenate([t1 * cos - t2 * sin, t2 * cos + t1 * sin], axis=-1)
    return out.astype(t.dtype)


def diff_attention(q, k, v, lam):
    seq = q.shape[1]
    q = q.transpose(0, 2, 3, 1, 4)
    k = k.transpose(0, 2, 3, 1, 4)
    v = v.transpose(0, 2, 1, 3)
    scale = QK_DIM ** -0.5
    outs = []
    for s0 in range(0, seq, Q_BLOCK):
        end = s0 + Q_BLOCK
        qb = q[:, :, :, s0:end]
        kb = k[:, :, :, :end]
        vb = v[:, :, :end]
        s = jnp.einsum('bhmqd,bhmkd->bhmqk', qb, kb).astype(jnp.float32) * scale
        mask = jnp.arange(end)[None, :] <= (s0 + jnp.arange(Q_BLOCK))[:, None]
        s = jnp.where(mask, s, -jnp.inf)
        p = jax.nn.softmax(s, axis=-1)
        w = p[:, :, 0] - lam * p[:, :, 1]
        outs.append(jnp.einsum('bhqk,bhkd->bhqd', w.astype(vb.dtype), vb))
    o = jnp.concatenate(outs, axis=2)
    return o.transpose(0, 2, 1, 3)


def causal_conv(x, w, b):
    c = x.shape[-1]
    y = lax.conv_general_dilated(
        x, w[:, None, :].astype(x.dtype), window_strides=(1,),
        padding=[(CONV_WIDTH - 1, 0)], dimension_numbers=('NWC', 'WIO', 'NWC'),
        feature_group_count=c)
    return y + b.astype(x.dtype)


def rg_lru(x, w_a, b_a, w_x, b_x, lru_lambda):
    bsz, seq, c = x.shape
    xb = x.reshape(bsz, seq, N_REC_BLOCKS, REC_BLOCK)
    r = jax.nn.sigmoid(jnp.einsum('bsni,nij->bsnj', xb, w_a).reshape(bsz, seq, c).astype(jnp.float32)
                       + b_a.astype(jnp.float32))
    i = jax.nn.sigmoid(jnp.einsum('bsni,nij->bsnj', xb, w_x).reshape(bsz, seq, c).astype(jnp.float32)
                       + b_x.astype(jnp.float32))
    log_a = -LRU_C * r * jax.nn.softplus(-lru_lambda.astype(jnp.float32))
    a = jnp.exp(log_a)
    mult = jnp.sqrt(-jnp.expm1(2.0 * log_a))
    u = mult * (i * x.astype(jnp.float32))

    def combine(left, right):
        a_l, u_l = left
        a_r, u_r = right
        return a_l * a_r, a_r * u_l + u_r

    _, h = lax.associative_scan(combine, (a, u), axis=1)
    return h.astype(x.dtype)


def setup_inputs(seed: int = 0) -> dict:
    key = jax.random.key(seed)
    ks = jax.random.split(key, 20)
    f32 = jnp.float32
    x = jax.random.normal(ks[0], (BATCH, SEQ, D_MODEL), f32)
    positions = (jnp.arange(SEQ, dtype=jnp.int32)[None, :]
                 + jax.random.randint(ks[1], (BATCH, 1), 0, 1024, dtype=jnp.int32))
    norm_gain = 1.0 + 0.02 * jax.random.normal(ks[2], (DEPTH, D_MODEL), f32)
    w_in = jax.random.normal(ks[3], (DEPTH, D_MODEL, IN_WIDTH), f32) * D_MODEL ** -0.5
    lambda_q1 = 0.1 * jax.random.normal(ks[4], (DEPTH, QK_DIM), f32)
    lambda_k1 = 0.1 * jax.random.normal(ks[5], (DEPTH, QK_DIM), f32)
    lambda_q2 = 0.1 * jax.random.normal(ks[6], (DEPTH, QK_DIM), f32)
    lambda_k2 = 0.1 * jax.random.normal(ks[7], (DEPTH, QK_DIM), f32)
    subln_gain = 1.0 + 0.02 * jax.random.normal(ks[8], (DEPTH, ATTN_HEAD_DIM), f32)
    conv_w = jax.random.normal(ks[9], (DEPTH, CONV_WIDTH, REC_WIDTH), f32) * CONV_WIDTH ** -0.5
    conv_b = 0.01 * jax.random.normal(ks[10], (DEPTH, REC_WIDTH), f32)
    w_a = jax.random.normal(ks[11], (DEPTH, N_REC_BLOCKS, REC_BLOCK, REC_BLOCK), f32) * REC_BLOCK ** -0.5
    b_a = 0.01 * jax.random.normal(ks[12], (DEPTH, REC_WIDTH), f32)
    w_x = jax.random.normal(ks[13], (DEPTH, N_REC_BLOCKS, REC_BLOCK, REC_BLOCK), f32) * REC_BLOCK ** -0.5
    b_x = 0.01 * jax.random.normal(ks[14], (DEPTH, REC_WIDTH), f32)
    a_c = jax.random.uniform(ks[15], (DEPTH, REC_WIDTH), f32, 0.9, 0.999)
    a_base = a_c ** (1.0 / LRU_C)
    lru_lambda = jnp.log(a_base) - jnp.log1p(-a_base)
    w_out = jax.random.normal(ks[16], (DEPTH, ATTN_WIDTH + REC_WIDTH, D_MODEL), f32) * (ATTN_WIDTH + REC_WIDTH) ** -0.5
    final_gain = 1.0 + 0.02 * jax.random.normal(ks[17], (D_MODEL,), f32)
    return {"x": x, "positions": positions, "norm_gain": norm_gain, "w_in": w_in,
            "lambda_q1": lambda_q1, "lambda_k1": lambda_k1, "lambda_q2": lambda_q2,
            "lambda_k2": lambda_k2, "subln_gain": subln_gain, "conv_w": conv_w,
            "conv_b": conv_b, "w_a": w_a, "b_a": b_a, "w_x": w_x, "b_x": b_x,
            "lru_lambda": lru_lambda, "w_out": w_out, "final_gain": final_gain}


def reference(x, positions, norm_gain, w_in, lambda_q1, lambda_k1, lambda_q2, lambda_k2,
              subln_gain, conv_w, conv_b, w_a, b_a, w_x, b_x, lru_lambda, w_out, final_gain):
    bsz, seq, _ = x.shape
    inv_freq = ROPE_THETA ** (-jnp.arange(0, QK_DIM, 2, dtype=jnp.float32) / QK_DIM)
    ang = positions.astype(jnp.float32)[..., None] * inv_freq
    cos = jnp.cos(ang)[:, :, None, None, :]
    sin = jnp.sin(ang)[:, :, None, None, :]
    for l in range(DEPTH):
        h = rms_norm(x, norm_gain[l])
        proj = h @ w_in[l]
        q, k, v, g_attn, xr, g_rec = jnp.split(proj, SPLITS, axis=-1)
        q = rope(q.reshape(bsz, seq, N_ATTN_HEADS, 2, QK_DIM), cos, sin)
        k = rope(k.reshape(bsz, seq, N_ATTN_HEADS, 2, QK_DIM), cos, sin)
        v = v.reshape(bsz, seq, N_ATTN_HEADS, ATTN_HEAD_DIM)
        lam_init = 0.8 - 0.6 * math.exp(-0.3 * l)
        lam = (jnp.exp(jnp.sum(lambda_q1[l].astype(jnp.float32) * lambda_k1[l].astype(jnp.float32)))
               - jnp.exp(jnp.sum(lambda_q2[l].astype(jnp.float32) * lambda_k2[l].astype(jnp.float32)))
               + lam_init)
        o = diff_attention(q, k, v, lam)
        o = rms_norm(o, subln_gain[l]) * (1.0 - lam_init)
        o = o.reshape(bsz, seq, ATTN_WIDTH) * jax.nn.silu(g_attn)
        r = causal_conv(xr, conv_w[l], conv_b[l])
        r = rg_lru(r, w_a[l], b_a[l], w_x[l], b_x[l], lru_lambda[l])
        r = r * jax.nn.silu(g_rec)
        mix = jnp.concatenate([o, r], axis=-1)
        x = x + mix @ w_out[l]
    return rms_norm(x, final_gain)
```

```python
import numpy as np
import concourse.bass as bass
import concourse.mybir as mybir

F32 = mybir.dt.float32
BF16 = mybir.dt.bfloat16
I32 = mybir.dt.int32
AF = mybir.ActivationFunctionType
ALU = mybir.AluOpType
AX = mybir.AxisListType

PAGE = 256
SB_BYTES = 224 * 1024
SB_BASE = 17 * 1024
PS_BYTES = 16 * 1024
DT_SIZE = {F32: 4, BF16: 2, I32: 4}


class View:
    __slots__ = ("ap", "space", "q0", "q1", "b0", "b1")

    def __init__(self, ap, space, q0, q1, b0, b1):
        self.ap, self.space, self.q0, self.q1, self.b0, self.b1 = ap, space, q0, q1, b0, b1


class Buf:
    def __init__(self, prog, name, shape, dtype, offset, space="sb", parts=128):
        self.prog, self.name, self.shape, self.dtype = prog, name, list(shape), dtype
        self.offset, self.space, self.parts = offset, space, parts
        self.esz = DT_SIZE[dtype]
        n = 1
        for s in shape:
            n *= s
        self.nbytes = n * self.esz
        if space == "sb":
            assert offset + self.nbytes <= SB_BYTES, (name, offset, self.nbytes)
            self.t = prog.nc.alloc_sbuf_tensor_at(name, [parts] + list(shape), dtype, offset=offset)
        else:
            self.t = None
        strides = []
        acc = 1
        for s in reversed(shape):
            strides.append(acc)
            acc *= s
        self.strides = list(reversed(strides))

    def __getitem__(self, idx):
        if not isinstance(idx, tuple):
            idx = (idx,)
        idx = list(idx)
        while len(idx) < 1 + len(self.shape):
            idx.append(slice(None))
        ps = idx[0]
        if isinstance(ps, slice):
            p0 = ps.start or 0
            p1 = self.parts if ps.stop is None else ps.stop
        else:
            p0, p1 = ps, ps + 1
        lo = 0
        hi = 0
        for k, (ix, n, st) in enumerate(zip(idx[1:], self.shape, self.strides)):
            if isinstance(ix, slice):
                a = ix.start or 0
                b = n if ix.stop is None else ix.stop
                step = ix.step or 1
                last = a + ((b - a - 1) // step) * step
            else:
                a, last = ix, ix
            lo += a * st
            hi += last * st
        b0 = self.offset + lo * self.esz
        b1 = self.offset + (hi + 1) * self.esz
        ap = self.t[tuple(idx)]
        return View(ap, self.space, p0 // 32, (p1 + 31) // 32, b0, b1)


class Dram:
    def __init__(self, key):
        self.space, self.key = "dram", key


class Op:
    __slots__ = ("eng", "fn", "waits", "sem", "inc", "val")


class Prog:
    ENGS = ("pe", "act", "dve", "pool", "sp")

    def __init__(self, nc):
        self.nc = nc
        self.ops = {e: [] for e in self.ENGS}
        self.sems = {}
        self.semcount = {}
        self.eng_sem = {}
        self.waited = {e: {} for e in self.ENGS}
        self.lastw = {}
        self.lastr = {}
        self.dram_w = {}
        self.dram_r = {}
        self._ctx = []
        for e in ("pe", "act", "dve", "pool"):
            self.eng_sem[e] = self.new_sem("e_" + e)
        self.psum_banks = []
        self.n_ops = 0

    def new_sem(self, name):
        g = self.nc.semaphore(name)
        s = g.__enter__()
        self._ctx.append(g)
        self.sems[name] = s
        self.semcount[name] = 0
        return name

    def psum(self, name, bank, shape=(512,), dtype=F32):
        b = Buf(self, name, shape, dtype, bank * 2048, space="ps")
        g = self.nc.psum_tensor(name, [128] + list(shape), dtype)
        b.t = g.__enter__()
        self._ctx.append(g)
        return b

    def _pages(self, v):
        pg = 2048 if v.space == "ps" else PAGE
        p0 = v.b0 // pg
        p1 = (v.b1 + pg - 1) // pg
        for q in range(v.q0, v.q1):
            for p in range(p0, p1):
                yield (v.space, q, p)

    def _need(self, need, d):
        for s, val in d.items():
            if need.get(s, 0) < val:
                need[s] = val

    def add(self, eng, fn, reads=(), writes=(), dma_sem=None, cc=False):
        op = Op()
        op.eng, op.fn = eng, fn
        if cc:
            op.sem, op.inc = dma_sem, None
            self.semcount[op.sem] += 1
        elif dma_sem is not None:
            op.sem, op.inc = dma_sem, 16
            self.semcount[op.sem] += 16
        else:
            op.sem, op.inc = self.eng_sem[eng], 1
            self.semcount[op.sem] += 1
        op.val = self.semcount[op.sem]
        ps_reads = [v for v in reads if v.space == "ps"]
        if ps_reads:
            reads = [v for v in reads if v.space != "ps"]
            writes = list(writes) + ps_reads
        need = {}
        for v in reads:
            if v.space == "dram":
                self._need(need, self.dram_w.get(v.key, {}))
            else:
                for k in self._pages(v):
                    d = self.lastw.get(k)
                    if d:
                        self._need(need, d)
        for v in writes:
            if v.space == "dram":
                self._need(need, self.dram_w.get(v.key, {}))
                self._need(need, self.dram_r.get(v.key, {}))
            else:
                for k in self._pages(v):
                    d = self.lastw.get(k)
                    if d:
                        self._need(need, d)
                    d = self.lastr.get(k)
                    if d:
                        self._need(need, d)
        w = self.waited[eng]
        waits = []
        engsems = set(self.eng_sem.values())
        for s, val in need.items():
            if eng == "pe" and s == self.eng_sem["pe"]:
                continue
            if s not in engsems:
                val = self.semcount[s] - (op.inc if (s == op.sem and op.inc) else 0)
                if s == op.sem and op.inc is None:
                    val = self.semcount[s] - 1
            if w.get(s, 0) >= val:
                continue
            w[s] = val
            waits.append((s, val))
        op.waits = waits
        tok = {op.sem: op.val}
        for v in writes:
            if v.space == "dram":
                self.dram_w[v.key] = dict(tok)
                self.dram_r[v.key] = {}
            else:
                for k in self._pages(v):
                    self.lastw[k] = tok
                    self.lastr[k] = {}
        for v in reads:
            if v.space == "dram":
                d = self.dram_r.setdefault(v.key, {})
                if d.get(op.sem, 0) < op.val:
                    d[op.sem] = op.val
            else:
                for k in self._pages(v):
                    d = self.lastr.get(k)
                    if d is None or len(d) == 0:
                        d = {}
                        self.lastr[k] = d
                    if d.get(op.sem, 0) < op.val:
                        d[op.sem] = op.val
        self.ops[eng].append(op)
        self.n_ops += 1
        return op

    def final_wait(self, eng, sems):
        waits = [(s, self.semcount[s]) for s in sems if self.semcount[s] > 0]
        op = Op()
        op.eng, op.fn, op.waits, op.sem, op.inc, op.val = eng, None, waits, None, 0, 0
        self.ops[eng].append(op)

    def emit(self):
        nc = self.nc
        engobj = {"pe": "tensor", "act": "scalar", "dve": "vector", "pool": "gpsimd", "sp": "sync"}
        with nc.Block() as block:
            for e in self.ENGS:
                ops = self.ops[e]
                if not ops:
                    continue

                def body(eng, ops=ops):
                    for op in ops:
                        for s, val in op.waits:
                            eng.wait_ge(self.sems[s], val)
                        if op.fn is None:
                            continue
                        ins = op.fn(eng)
                        if op.inc is None:
                            ins.then_inc(self.sems[op.sem])
                        else:
                            ins.then_inc(self.sems[op.sem], op.inc)

                getattr(block, engobj[e])(body)

    def close(self):
        for g in reversed(self._ctx):
            g.__exit__(None, None, None)


import ml_dtypes
from concourse.bass_utils import run_bass_kernel_spmd

S = 2048
D = 2048
NH = 4
NRB = 4
EPS = 1e-6
LAM_INIT = 0.2
TWO_PI_SCALE = 6.28318
WARM_DUP = 0
PAIRS = [[0, 1], [2, 3], [4, 5], [6, 7]]


class Arena:
    def __init__(self, lo, hi):
        self.lo, self.hi, self.off = lo, hi, lo

    def take(self, nbytes):
        nbytes = (nbytes + 255) // 256 * 256
        o = self.off
        self.off += nbytes
        assert self.off <= self.hi, ("arena overflow", self.off, self.hi)
        return o


def build_program(PAIRS=PAIRS, STOP=99):
    nc = bass.Bass("TRN2", target_bir_lowering=False)
    dt = nc.dram_tensor
    x_d = dt("x", [S, D], F32, kind="ExternalInput")
    xres_d = dt("xres", [S, 1024], F32, kind="ExternalInput")
    pos_d = dt("pos", [1, S], I32, kind="ExternalInput")
    ng_d = dt("ng", [1, D], F32, kind="ExternalInput")
    wv_d = dt("wv", [128, 16 * 512], F32, kind="ExternalInput")
    wct_d = dt("wct", [20, 128, 16 * 128], F32, kind="ExternalInput")
    wo_d = dt("wo", [16, 128, 1024], F32, kind="ExternalInput")
    lam_d = dt("lamv", [1, 256], F32, kind="ExternalInput")
    vec_d = dt("vecs", [128, 40], F32, kind="ExternalInput")
    fg_d = dt("fg", [1, 1024], F32, kind="ExternalInput")
    wa_d = dt("wa", [128, 4 * 128], F32, kind="ExternalInput")
    wx_d = dt("wx", [128, 4 * 128], F32, kind="ExternalInput")
    cbf_d = dt("cbf", [128, 384 + 1024], BF16, kind="ExternalInput")
    y_d = dt("y", [S, 1024], F32, kind="ExternalOutput")
    mixR_in = dt("mixR_in", [4 * 128, S], BF16)
    mixR_out = dt("mixR_out", [8 * 128, S], BF16)
    mixA1_in = dt("mixA1_in", [3 * 128, S], BF16)
    mixA1_out = dt("mixA1_out", [6 * 128, S], BF16)
    mixA2_in = dt("mixA2_in", [1 * 128, S], BF16)
    mixA2_out = dt("mixA2_out", [2 * 128, S], BF16)
    ss_in_h = [dt("ss_in%d" % i, [128, 8], F32) for i in range(4)]
    ss_out_h = [dt("ss_out%d" % i, [256, 8], F32) for i in range(4)]

    P = Prog(nc)
    ar = Arena(SB_BASE, SB_BYTES)

    def sb(name, shape, dtype, arena=None):
        a = arena or ar
        n = int(np.prod(shape)) * DT_SIZE[dtype]
        return Buf(P, name, shape, dtype, a.take(n))

    def _v(x):
        return x.ap if isinstance(x, View) else x

    def ACT(out, in_, func, bias=None, scale=None, accum=None, extra_reads=()):
        reads = [in_] + list(extra_reads)
        writes = [out]
        kw = {}
        if bias is not None:
            kw["bias"] = _v(bias)
            if isinstance(bias, View):
                reads.append(bias)
        if scale is not None:
            kw["scale"] = _v(scale)
            if isinstance(scale, View):
                reads.append(scale)
        if accum is not None:
            kw["accum_out"] = accum.ap
            writes.append(accum)
        P.add("act", lambda e: e.activation(out.ap, in_.ap, func, **kw), reads=reads, writes=writes)

    def TS(eng, out, in0, s1, s2, op0, op1=None):
        reads = [in0] + [s for s in (s1, s2) if isinstance(s, View)]
        if op1 is None:
            P.add(eng, lambda e: e.tensor_scalar(out.ap, in0.ap, _v(s1), None, op0), reads=reads, writes=[out])
        else:
            P.add(eng, lambda e: e.tensor_scalar(out.ap, in0.ap, _v(s1), _v(s2), op0, op1), reads=reads, writes=[out])

    def TT(eng, out, in0, in1, op):
        P.add(eng, lambda e: e.tensor_tensor(out.ap, in0.ap, in1.ap, op), reads=[in0, in1], writes=[out])

    def STT(out, in0, sc, in1, op0, op1):
        reads = [in0, in1] + ([sc] if isinstance(sc, View) else [])
        P.add("dve", lambda e: e.scalar_tensor_tensor(out.ap, in0.ap, _v(sc), in1.ap, op0, op1), reads=reads, writes=[out])

    def RECIP(out, in_):
        P.add("dve", lambda e: e.reciprocal(out.ap, in_.ap), reads=[in_], writes=[out])

    def COPY(eng, out, in_):
        if eng == "act":
            P.add("act", lambda e: e.activation(out.ap, in_.ap, AF.Copy), reads=[in_], writes=[out])
        else:
            P.add(eng, lambda e: e.tensor_copy(out.ap, in_.ap), reads=[in_], writes=[out])

    def MEMSET(eng, out, val):
        P.add(eng, lambda e: e.memset(out.ap, val), writes=[out])

    def MMG(out, items, extra_reads=()):
        reads = list(extra_reads)
        for (l, r) in items:
            reads.append(l)
            reads.append(r)
        n = len(items)

        def fn(e):
            ins = None
            for k, (l, r) in enumerate(items):
                ins = e.matmul(out.ap, l.ap, r.ap, start=(k == 0), stop=(k == n - 1))
            return ins
        P.add("pe", fn, reads=reads, writes=[out])

    def MM(out, l, r, start=True, stop=True):
        P.add("pe", lambda e: e.matmul(out.ap, l.ap, r.ap, start=start, stop=stop), reads=[l, r], writes=[out])

    def DMA(eng, out, in_, sem, reads=(), writes=()):
        P.add(eng, lambda e: e.dma_start(out=_v(out), in_=_v(in_)), reads=list(reads), writes=list(writes), dma_sem=sem)

    sem_id = [0]

    def newsem(prefix="s"):
        sem_id[0] += 1
        return P.new_sem("%s%d" % (prefix, sem_id[0]))

    hT = sb("hT", [16, S], BF16)
    A_off = hT.offset
    Vt = sb("Vt", [16, 512], BF16)
    B_off = Vt.offset
    cosT = sb("cosT", [S], F32)
    sinT = sb("sinT", [S], F32)
    C_off = cosT.offset
    wbuf = [sb("wbuf%d" % i, [16, 128], BF16) for i in range(3)]
    W_off = wbuf[0].offset
    cbf = sb("cbf", [384 + 1024], BF16)
    ident = cbf[:, 0:128]
    perm = cbf[:, 128:256]
    onesb = cbf[:, 256:384]
    maskA = cbf[:, 384:896]
    maskB = cbf[:, 896:1408]
    ones32 = sb("ones32", [128], F32)
    vecs = sb("vecs", [40], F32)
    lamb = sb("lamb", [256], F32)
    sm = sb("sm", [64], F32)
    ssA = sb("ssA", [16], F32)
    sdA = sb("sdA", [16], F32)
    rsA = sb("rsA", [16], F32)
    sp8 = sb("sp8", [4], F32)
    mhalf16 = sb("mhalf16", [16], F32)
    sp16 = sb("sp16", [4], F32)
    wab = sb("wab", [512], BF16)
    wxb = sb("wxb", [512], BF16)
    eps_t = sm[:, 0:1]
    one_t = sm[:, 1:2]
    nlam = sm[:, 2:3]
    gsub = sm[:, 3:4]
    dyn_lo = ar.off
    dyn_hi = SB_BYTES - 256

    accb = [P.psum("acc0", 0), P.psum("acc1", 1)]
    rps = P.psum("rps", 2)
    mps = P.psum("mps", 3)
    Sps = [P.psum("S0", 4), P.psum("S1", 5), rps]
    Ops = P.psum("Ops", 6)
    Zps = P.psum("Zps", 7)

    def ps_bf16_view(buf):
        ap = buf.t[:, 0:512].bitcast(BF16).rearrange("p (c k) -> p c k", c=8)
        return View(ap, "ps", 0, 4, buf.offset, buf.offset + 2048)

    s_c = newsem("c")
    DMA("sp", cbf[:], cbf_d[:, :], s_c, writes=[cbf[:]])
    s_c2 = newsem("c")
    DMA("sp", vecs[:], vec_d[:, :], s_c2, writes=[vecs[:]])
    s_c3 = newsem("c")
    DMA("sp", lamb[:], lam_d[0:1, :].partition_broadcast(128), s_c3, writes=[lamb[:]])
    s_c4 = newsem("c")
    DMA("pool", wab[:], wa_d[:, :], s_c4, writes=[wab[:]])
    s_c5 = newsem("c")
    DMA("pool", wxb[:], wx_d[:, :], s_c5, writes=[wxb[:]])
    MEMSET("dve", ones32[:], 1.0)
    MEMSET("dve", eps_t, EPS)
    MEMSET("dve", one_t, 1.0)
    lt = sb("lt", [2, 64], F32)
    TT("dve", lt[:, 0, :], lamb[:, 0:64], lamb[:, 64:128], ALU.mult)
    TT("dve", lt[:, 1, :], lamb[:, 128:192], lamb[:, 192:256], ALU.mult)
    P.add("dve", lambda e: e.tensor_reduce(sm[:, 4:6].ap, lt[:].ap, AX.X, ALU.add), reads=[lt[:]], writes=[sm[:, 4:6]])
    ACT(sm[:, 6:8], sm[:, 4:6], AF.Exp)
    TT("dve", sm[:, 8:9], sm[:, 7:8], sm[:, 6:7], ALU.subtract)
    TS("dve", nlam, sm[:, 8:9], -LAM_INIT, None, ALU.add)
    TS("dve", gsub, vecs[:, 2:3], 1.0 - LAM_INIT, None, ALU.mult)
    ACT(sm[:, 12:16], vecs[:, 32:36], AF.Exp, scale=-1.0)
    ACT(sm[:, 16:20], sm[:, 12:16], AF.Ln, bias=one_t, scale=1.0)
    TS("dve", sp8[:], sm[:, 16:20], -8.0, None, ALU.mult)
    TS("dve", sp16[:], sm[:, 16:20], -16.0, None, ALU.mult)

    if STOP == 0:
        P.final_wait('sp', [k for k in P.semcount if k not in P.eng_sem.values()]); P.final_wait('pool', list(P.eng_sem.values())); P.emit(); P.close(); return nc
    dyn = Arena(dyn_lo, dyn_hi)
    wvb = sb("wvb", [16, 512], BF16, dyn)
    s_wv = newsem("w")

    def load_wv():
        for q4 in range(4):
            DMA("pool", wvb[:, q4 * 4:(q4 + 1) * 4, :], wv_d[:, q4 * 2048:(q4 + 1) * 2048].rearrange("p (c k) -> p c k", k=512), s_wv,
                writes=[wvb[:, q4 * 4:(q4 + 1) * 4, :]])
    xst = [sb("xst%d" % i, [D], F32, dyn) for i in range(4)]
    junk = sb("junk", [D], BF16, dyn)
    xn = [sb("xn%d" % i, [D], BF16, dyn) for i in range(3)]
    gbc = sb("gbc", [D], F32, dyn)
    mhalf = sm[:, 10:11]
    MEMSET("pool", mhalf, -0.5)
    MEMSET("pool", mhalf16[:], -0.5)
    s_g = newsem("c")
    DMA("sp", gbc[:], ng_d[0:1, :].partition_broadcast(128), s_g, writes=[gbc[:]])
    s_x = [newsem("x") for _ in range(4)]

    def stage1(t):
        xt = xst[t % 4]
        DMA("sp", xt[:], x_d[t * 128:(t + 1) * 128, :], s_x[t % 4], writes=[xt[:]])
        ACT(junk[:], xt[:], AF.Square, accum=ssA[:, t:t + 1])
        TS("pool", sdA[:, t:t + 1], ssA[:, t:t + 1], 1.0 / D, EPS, ALU.mult, ALU.add)
        TT("pool", rsA[:, t:t + 1], sdA[:, t:t + 1], mhalf, ALU.pow)
        STT(xn[t % 3][:], xt[:], rsA[:, t:t + 1], gbc[:], ALU.mult, ALU.mult)

    def stage2(t):
        for half in range(2):
            pb = accb[half]
            pv = ps_bf16_view(pb)
            xnb = xn[t % 3]

            def fn(e, pb=pb, xnb=xnb, half=half):
                ins = None
                bf = pb.t[:, 0:512].bitcast(BF16)
                for c in range(8):
                    cc = half * 8 + c
                    ins = e.transpose(bf[:, c * 128:(c + 1) * 128], xnb.t[:, cc * 128:(cc + 1) * 128], ident.ap)
                return ins
            P.add("pe", fn, reads=[xnb[:], ident], writes=[pv])
            dst = hT[:, half * 8:(half + 1) * 8, t * 128:(t + 1) * 128]
            if half == 0:
                P.add("act", lambda e, pv=pv, dst=dst: e.activation(dst.ap, pv.ap, AF.Copy), reads=[pv], writes=[dst])
            else:
                P.add("dve", lambda e, pv=pv, dst=dst: e.tensor_copy(dst.ap, pv.ap), reads=[pv], writes=[dst])

    vbanks = [rps, mps]

    def stage3(t):
        acc = vbanks[t % 2]
        MMG(acc[:], [(hT[:, c, t * 128:(t + 1) * 128], wvb[:, c, :]) for c in range(16)])
        COPY("act", Vt[:, t, :], acc[:])

    for t in range(16 + 2):
        if t < 16:
            stage1(t)
        if t == 4:
            load_wv()
        if 0 <= t - 2 < 16:
            stage2(t - 2)
    dyn = Arena(dyn_lo + 16384, dyn_hi)
    posi = sb("posi", [S], I32, dyn)
    tt_ = sb("ttab", [S], F32, dyn)
    ti_ = sb("titab", [S], I32, dyn)
    tf_ = sb("tftab", [S], F32, dyn)
    tf2_ = sb("tf2tab", [S], F32, dyn)
    s_p = newsem("c")
    DMA("sp", posi[:], pos_d[0:1, :].partition_broadcast(128), s_p, writes=[posi[:]])
    COPY("dve", tf_[:], posi[:])
    TS("dve", tt_[:], tf_[:], vecs[:, 0:1], None, ALU.mult)
    COPY("dve", ti_[:], tt_[:])
    COPY("dve", tf_[:], ti_[:])
    TT("dve", tf_[:], tt_[:], tf_[:], ALU.subtract)
    TS("dve", tt_[:], tt_[:], 0.25, None, ALU.add)
    COPY("dve", ti_[:], tt_[:])
    COPY("dve", tf2_[:], ti_[:])
    TT("dve", tf2_[:], tt_[:], tf2_[:], ALU.subtract)
    for t in range(16):
        stage3(t)
        if t == 6:
            ACT(sinT[:], tf_[:], AF.Sin, scale=TWO_PI_SCALE)
        if t == 11:
            ACT(cosT[:], tf2_[:], AF.Sin, scale=TWO_PI_SCALE)
    TS("dve", sinT[:], sinT[:], vecs[:, 1:2], None, ALU.mult)

    if STOP == 2:
        P.final_wait('sp', [k for k in P.semcount if k not in P.eng_sem.values()]); P.final_wait('pool', list(P.eng_sem.values())); P.emit(); P.close(); return nc
    s_w = [newsem("w") for _ in range(3)]
    wstate = {"next": 0}

    def w_prefetch(upto):
        while wstate["next"] <= min(upto, 19):
            ct = wstate["next"]
            DMA("pool", wbuf[ct % 3][:], wct_d[ct, :, :].rearrange("p (c k) -> p c k", k=128), s_w[ct % 3], writes=[wbuf[ct % 3][:]])
            wstate["next"] += 1

    gcount = [0]

    acc_banks = [accb[0], accb[1]]

    def inproj_group(ct, tb):
        acc = acc_banks[gcount[0] % len(acc_banks)]
        gcount[0] += 1
        wb = wbuf[ct % 3]
        MMG(acc[:], [(wb[:, c, :], hT[:, c, tb * 512:(tb + 1) * 512]) for c in range(16)])
        return acc

    dyn = Arena(dyn_lo, dyn_hi)
    hsets = []
    for i in range(2):
        qz = sb("qz%d" % i, [8, 2, 256], BF16, dyn)
        kT = sb("kT%d" % i, [S], BF16, dyn)
        gs = sb("gs%d" % i, [S], BF16, dyn)
        mixh = sb("mixh%d" % i, [S], BF16, dyn)
        hsets.append((qz, kT, gs, mixh))
    ET = [sb("ET%d" % i, [512], BF16, dyn) for i in range(3)]
    rsb = sb("rsb", [512], F32, dyn)
    tbuf = sb("tbuf", [512], F32, dyn)
    obufs = [sb("obuf%d" % i, [256], F32, dyn) for i in range(2)]
    osqs = [sb("osq%d" % i, [256], F32, dyn) for i in range(2)]
    sdbs = [sb("sdb%d" % i, [256], F32, dyn) for i in range(2)]
    rstds = [sb("rstd%d" % i, [256], F32, dyn) for i in range(2)]
    onbs = [sb("onb%d" % i, [256], F32, dyn) for i in range(2)]
    head_end = dyn.off
    tail = Arena(dyn_hi - 10 * 1024, dyn_hi)
    qbf = [sb("qbf%d" % i, [512], BF16, tail) for i in range(2)]
    t1b = [sb("t1b%d" % i, [2, 256], F32, tail) for i in range(2)]
    t2b = [sb("t2b%d" % i, [2, 256], F32, tail) for i in range(2)]
    TAIL_LO = dyn_hi - 10 * 1024
    WO_END = head_end + 4096
    rdyn = Arena(dyn_lo, TAIL_LO)
    xpads = [sb("xpad%d" % i, [S + 3], F32, rdyn) for i in range(2)]
    cvbs = [sb("cvb%d" % i, [S], BF16, rdyn) for i in range(2)]
    gsrs = [sb("gsr%d" % i, [S], BF16, rdyn) for i in range(3)]
    cvs = [sb("cv%d" % i, [S], F32, rdyn) for i in range(2)]
    rgf = sb("rgf", [S], F32, rdyn)
    igf = sb("igf", [S], F32, rdyn)
    afull = sb("afull", [S], F32, rdyn)
    mixr = sb("mixr", [S], BF16, rdyn)


    rope_cnt = [0]

    rope_pending = []

    def flush_rope():
        while rope_pending:
            rope_pending.pop(0)()

    def evac_rope(acc, tb, kind, hs):
        i = rope_cnt[0] % 2
        rope_cnt[0] += 1
        tok = slice(tb * 512, (tb + 1) * 512)
        COPY("act", qbf[i][:], acc[:])
        t1v = t1b[i][:]
        t2v = t2b[i][:]
        P.add("dve", lambda e, a=acc, o=t1b[i]: e.tensor_tensor(o.t[:, :, :].rearrange("p a b -> p (a b)"), a.t[:, 0:512], cosT.t[:, tok], ALU.mult),
              reads=[acc[:], cosT[:, tok]], writes=[t1v])

        def part2():
            MM(mps[:], perm, qbf[i][:])
            P.add("dve", lambda e, o=t2b[i]: e.tensor_tensor(o.t[:, :, :].rearrange("p a b -> p (a b)"), mps.t[:, 0:512], sinT.t[:, tok], ALU.mult),
                  reads=[mps[:], sinT[:, tok]], writes=[t2v])
            if kind == "q":
                qz = hs[0]
                for m in range(2):
                    ps = slice(64 * m, 64 * (m + 1))
                    TT("pool", qz[ps, 2 * tb:2 * tb + 2, m, :], t1b[i][ps, :, :], t2b[i][ps, :, :], ALU.add)
            else:
                kT = hs[1]
                P.add("pool", lambda e, o=kT, a=t1b[i], b=t2b[i]: e.tensor_tensor(
                    o.t[:, tok], a.t[:, :, :].rearrange("p a b -> p (a b)"), b.t[:, :, :].rearrange("p a b -> p (a b)"), ALU.add),
                    reads=[t1v, t2v], writes=[kT[:, tok]])
        rope_pending.append(part2)

    def head_inproj_tasks(h, hs):
        tasks = []
        base = 8 + 3 * h
        for j, kind in enumerate(("q", "k", "g")):
            ct = base + j
            for tb in range(4):
                def task(ct=ct, tb=tb, kind=kind):
                    if tb == 0:
                        w_prefetch(ct + 2)
                    acc = inproj_group(ct, tb)
                    flush_rope()

                    def evac(acc=acc):
                        if kind == "g":
                            ACT(hs[2][:, tb * 512:(tb + 1) * 512], acc[:], AF.Silu)
                        else:
                            evac_rope(acc, tb, kind, hs)
                    return evac
                tasks.append(task)
        return tasks

    s_mr = newsem("m")
    w_prefetch(1)
    h0_tasks = None
    for i in range(2):
        MEMSET("dve", xpads[i][:, 0:3], 0.0)
    gps = [rps, mps]

    def rec_inproj(n):
        xpad, gsr = xpads[n % 2], gsrs[n % 3]
        ctx_, ctg = 2 * n, 2 * n + 1
        w_prefetch(ctx_ + 2)
        for tb in range(4):
            acc = inproj_group(ctx_, tb)
            COPY("act", xpad[:, 3 + tb * 512:3 + (tb + 1) * 512], acc[:])
        w_prefetch(ctg + 2)
        for tb in range(4):
            acc = inproj_group(ctg, tb)
            ACT(gsr[:, tb * 512:(tb + 1) * 512], acc[:], AF.Silu)

    def rec_conv(n):
        xpad, cv, cvb = xpads[n % 2], cvs[n % 2], cvbs[n % 2]
        cw = lambda j: vecs[:, 4 + n * 4 + j:5 + n * 4 + j]
        TS("dve", cv[:], xpad[:, 0:S], cw(0), vecs[:, 20 + n:21 + n], ALU.mult, ALU.add)
        for j in range(1, 4):
            STT(cv[:], xpad[:, j:j + S], cw(j), cv[:], ALU.mult, ALU.add)
        COPY("dve", cvb[:], cv[:])

    def rec_gates(n):
        cvb = cvbs[n % 2]
        for tb in range(4):
            tok = slice(tb * 512, (tb + 1) * 512)
            MM(gps[0][:], wab[:, n * 128:(n + 1) * 128], cvb[:, tok])
            ACT(rgf[:, tok], gps[0][:], AF.Sigmoid, bias=vecs[:, 24 + n:25 + n], scale=1.0)
            MM(gps[1][:], wxb[:, n * 128:(n + 1) * 128], cvb[:, tok])
            ACT(igf[:, tok], gps[1][:], AF.Sigmoid, bias=vecs[:, 28 + n:29 + n], scale=1.0)

    def rec_post(n, part=None):
        cv, gsr = cvs[n % 2], gsrs[n % 3]
        if part in (None, "a"):
            TT("dve", igf[:], igf[:], cv[:], ALU.mult)
            ACT(afull[:], rgf[:], AF.Exp, scale=sp8[:, n:n + 1])
            ACT(rgf[:], rgf[:], AF.Exp, scale=sp16[:, n:n + 1])
            ACT(rgf[:], rgf[:], AF.Ln, bias=one_t, scale=-1.0)
            ACT(rgf[:], rgf[:], AF.Exp, scale=0.5)
        if part == "a":
            return
        TT("dve", igf[:], igf[:], rgf[:], ALU.mult)
        P.add("dve", lambda e: e.tensor_tensor_scan(rgf[:].ap, afull[:].ap, igf[:].ap, 0.0, ALU.mult, ALU.add),
              reads=[afull[:], igf[:]], writes=[rgf[:]])
        TT("dve", mixr[:], rgf[:], gsr[:], ALU.mult)
        DMA("sp", mixR_in[n * 128:(n + 1) * 128, :], mixr[:], s_mr, reads=[mixr[:]], writes=[Dram("mixR_in")])

    rec_inproj(0)
    rec_conv(0)
    rec_inproj(1)
    for n in range(NRB):
        rec_gates(n)
        if n + 1 < NRB:
            rec_conv(n + 1)
        if n + 2 < NRB:
            rec_post(n)
            rec_inproj(n + 2)
            continue
        if n == NRB - 2:
            rec_post(n)
            h0_tasks = head_inproj_tasks(0, hsets[0])
            MEMSET("pool", hsets[0][0][:], 0.0)
            acc_banks.extend([Sps[0], Sps[1], Ops, Zps])
            for tk in h0_tasks[0:4]:
                tk()()
            continue
        if n == NRB - 1:
            rec_post(n, "a")
            for tk in h0_tasks[4:12]:
                tk()()
            flush_rope()
            rec_post(n, "b")
            del acc_banks[2:]
            gcount[0] = 0
            continue
        rec_post(n)

    if STOP == 3:
        P.final_wait('sp', [k for k in P.semcount if k not in P.eng_sem.values()]); P.final_wait('pool', list(P.eng_sem.values())); P.emit(); P.close(); return nc
    s_cc1 = newsem("cc")

    def cc_op(kind_in, kind_out, tin, tout, sem):
        op = P.add("pool", lambda e: e.collective_compute("AllGather", ALU.bypass, replica_groups=PAIRS,
                                                          ins=[tin.ap().opt()], outs=[tout.ap().opt()]),
                   reads=[Dram(kind_in)], writes=[Dram(kind_out)], dma_sem=sem, cc=True)
    cc_op("mixR_in", "mixR_out", mixR_in, mixR_out, s_cc1)

    if STOP == 4:
        P.final_wait('sp', [k for k in P.semcount if k not in P.eng_sem.values()]); P.final_wait('pool', list(P.eng_sem.values())); P.emit(); P.close(); return nc
    MEMSET("dve", hsets[1][0][:], 0.0)
    if STOP == 401:
        P.final_wait('sp', [k for k in P.semcount if k not in P.eng_sem.values()]); P.final_wait('pool', list(P.eng_sem.values())); P.emit(); P.close(); return nc
    if STOP == 41:
        P.final_wait('sp', [k for k in P.semcount if k not in P.eng_sem.values()]); P.final_wait('pool', list(P.eng_sem.values())); P.emit(); P.close(); return nc
    epi_cnt = [0]
    s_ma = newsem("m")
    s_ma2 = newsem("m")
    s_cc2a = newsem("cc")
    pair_cnt = [0]
    srcs = [(mixR_out, "mixR_out", 0, 4, 0), (mixR_out, "mixR_out", 1, 4, 4), (mixA1_out, "mixA1_out", 0, 3, 8),
            (mixA1_out, "mixA1_out", 1, 3, 11), (mixA2_out, "mixA2_out", 0, 1, 14), (mixA2_out, "mixA2_out", 1, 1, 15)]

    def chunk_src(c):
        for (src, key, slot, nch, c0) in srcs:
            if c0 <= c < c0 + nch:
                return src, key, (slot * nch + (c - c0)) * 128
    mixc = [None] * 16
    early_offs = [WO_END + i * 4096 for i in range(4)] + [TAIL_LO + i * 4096 for i in range(2)] + [dyn_lo + i * 4096 for i in range(5)]
    assert WO_END + 4 * 4096 <= TAIL_LO
    for c in range(11):
        mixc[c] = Buf(P, "mixc%d" % c, [S], BF16, early_offs[c])
    s_mxe = newsem("mx")

    def load_mix_early():
        for c in range(11):
            src, key, r0 = chunk_src(c)
            DMA("sp", mixc[c][:], src.ap()[r0:r0 + 128, :], s_mxe, reads=[Dram(key)], writes=[mixc[c][:]])

    def load_wo():
        wo_bufs = []
        carena = Arena(C_off, C_off + 16384)
        for c in range(8):
            wo_bufs.append(sb("wo%d" % c, [1024], BF16, carena))
        warena = Arena(W_off, W_off + 12288)
        for c in range(8, 14):
            wo_bufs.append(sb("wo%d" % c, [1024], BF16, warena))
        odyn = Arena(head_end, head_end + 4096)
        for c in range(14, 16):
            wo_bufs.append(sb("wo%d" % c, [1024], BF16, odyn))
        s_wo = newsem("w")
        for c in range(16):
            DMA("pool", wo_bufs[c][:], wo_d[c, :, :], s_wo, writes=[wo_bufs[c][:]])
        return wo_bufs

    wo_box = [None]
    deferred = []
    evac_pending = []
    nxt_box = [[]]
    gpairs = [(h, j, kt) for h in range(NH) for j in range(8) for kt in range(2 * (j + 1))]
    ngp = len(gpairs)

    def issue_S(gi):
        h, j, kt = gpairs[gi]
        qz, kT, gs, mixh = hsets[h % 2]
        Sb = Sps[gi % 3]
        qv = qz[:, j, :, :]
        ktv = kT[:, kt * 128:(kt + 1) * 128]
        if kt == 2 * j + 1:
            def fnb(e, Sb=Sb, qz=qz, kT=kT, j=j, kt=kt):
                ins = None
                for m in range(2):
                    o2 = Sb.t[:, m * 256 + 128:m * 256 + 256]
                    e.matmul(o2, kT.t[:, kt * 128:(kt + 1) * 128], qz.t[:, j, m, 128:256], start=True, stop=False)
                    ins = e.matmul(o2, ident.ap, cbf.t[:, 896 + m * 256 + 128:896 + m * 256 + 256], start=False, stop=True)
                return ins
            P.add("pe", fnb, reads=[ktv, qv, ident, maskB], writes=[Sb[:]])
            return
        diag = (kt == 2 * j)

        def fn(e, Sb=Sb, qz=qz, kT=kT, j=j, kt=kt, diag=diag):
            ins = e.matmul(Sb.t[:, 0:512], kT.t[:, kt * 128:(kt + 1) * 128],
                           qz.t[:, j, :, :].rearrange("p a b -> p (a b)"), start=True, stop=(not diag))
            if diag:
                ins = e.matmul(Sb.t[:, 0:512], ident.ap, maskA.ap, start=False, stop=True)
            return ins
        P.add("pe", fn, reads=[ktv, qv] + ([ident, maskA] if diag else []), writes=[Sb[:]])

    followups = []
    followups_a = []
    release_pending = []

    def flush_followups():
        while followups:
            followups.pop(0)()

    def pop_tasks(j):
        n = (1, 1, 1, 1, 1, 1, 2, 4)[j]
        nxt = nxt_box[0]
        for _ in range(n):
            if nxt:
                while len(evac_pending) >= 2:
                    evac_pending.pop(0)()
                evac_pending.append(nxt.pop(0)())

    def finish_head(h):
        mixh = hsets[h % 2][3]
        if h < 3:
            DMA("sp", mixA1_in[h * 128:(h + 1) * 128, :], mixh[:], s_ma, reads=[mixh[:]], writes=[Dram("mixA1_in")])
        else:
            DMA("sp", mixA2_in[0:128, :], mixh[:], s_ma2, reads=[mixh[:]], writes=[Dram("mixA2_in")])
        if h == 2:
            cc_op("mixA1_in", "mixA1_out", mixA1_in, mixA1_out, s_cc2a)
            load_mix_early()

    def issue_rest(gi):
        h, j, kt = gpairs[gi]
        qz, kT, gs, mixh = hsets[h % 2]
        nkt = 2 * (j + 1)
        Sb = Sps[gi % 3]
        E = ET[gi % 3]
        if kt == 2 * j + 1:
            def v3(t):
                return t[:, 0:512].rearrange("p (m x) -> p m x", m=2)[:, :, 128:256]
            P.add("act", lambda e, Sb=Sb, E=E: e.activation(v3(E.t), v3(Sb.t), AF.Exp, scale=0.125),
                  reads=[Sb[:]], writes=[E[:]])
            Vv = Vt[:, kt, h * 128:(h + 1) * 128]

            def c2(t, m):
                return t[:, m * 256 + 128:m * 256 + 256]

            def fpv(e, E=E, Vv=Vv):
                e.matmul(c2(Ops.t, 0), Vv.ap, c2(E.t, 0), start=False, stop=False)
                return e.matmul(c2(Ops.t, 1), Vv.ap, c2(E.t, 1), start=False, stop=True)

            def fz(e, E=E):
                e.matmul(c2(Zps.t, 0), onesb.ap, c2(E.t, 0), start=False, stop=False)
                return e.matmul(c2(Zps.t, 1), onesb.ap, c2(E.t, 1), start=False, stop=True)
            P.add("pe", fpv, reads=[Vv, E[:]], writes=[Ops[:]])
            P.add("pe", fz, reads=[onesb, E[:]], writes=[Zps[:]])
        else:
            ACT(E[:], Sb[:], AF.Exp, scale=0.125)
            while release_pending:
                release_pending.pop(0)()
            MM(Ops[:], Vt[:, kt, h * 128:(h + 1) * 128], E[:], start=(kt == 0), stop=(kt == nkt - 1))
            MM(Zps[:], onesb, E[:], start=(kt == 0), stop=(kt == nkt - 1))
        if kt == nkt - 1:
            COPY("dve", tbuf[:], Ops[:])
            release_pending.append(lambda: ACT(rsb[:], Zps[:], AF.Ln))
        while followups_a:
            followups_a.pop(0)()
        flush_followups()
        while evac_pending:
            evac_pending.pop(0)()
        flush_rope()
        for d_ in list(deferred):
            d_[0] -= 1
            if d_[0] <= 0 and deferred and deferred[0] is d_:
                r_ = deferred.pop(0)[1]()
                if r_ is not None:
                    followups.append(r_)
        if kt != nkt - 1:
            return
        qs = slice(j * 256, (j + 1) * 256)
        eb = epi_cnt[0] % 2
        epi_cnt[0] += 1
        while len(deferred) > 0 and deferred[0][2] <= epi_cnt[0] - 2:
            r_ = deferred.pop(0)[1]()
            if r_ is not None:
                followups.append(r_)
        if followups and epi_cnt[0] >= 2:
            flush_followups()
        obuf, osq, sdb, rstd, onb = obufs[eb], osqs[eb], sdbs[eb], rstds[eb], onbs[eb]
        pop_tasks(j)
        def stage_b(qs=qs, gs=gs, mixh=mixh, obuf=obuf, osq=osq, sdb=sdb, rstd=rstd, onb=onb):
            MM(mps[:, 0:256], ones32[:], osq[:])

            def stage_c():
                ACT(sdb[:], mps[:, 0:256], AF.Ln, bias=eps_t, scale=1.0 / 128.0)
                ACT(rstd[:], sdb[:], AF.Exp, scale=-0.5)
                STT(onb[:], obuf[:], gsub, rstd[:], ALU.mult, ALU.mult)
                TT("pool", mixh[:, qs], onb[:], gs[:, qs], ALU.mult)
            return stage_c

        def stage_a(j=j, h=h, obuf=obuf, osq=osq, stage_b=stage_b, cnt=epi_cnt[0]):
            ACT(rsb[:], rsb[:], AF.Exp, scale=-1.0)
            TT("dve", tbuf[:], tbuf[:], rsb[:], ALU.mult)
            STT(obuf[:], tbuf[:, 256:512], nlam, tbuf[:, 0:256], ALU.mult, ALU.add)
            TT("pool", osq[:], obuf[:], obuf[:], ALU.mult)
            deferred.append([13, stage_b, cnt])
            if j == 7:
                deferred.append([14, lambda h=h: (flush_followups(), finish_head(h), None)[2], cnt])
        followups_a.append(stage_a)

    issue_S(0)
    issue_S(1)
    for gi in range(ngp):
        h, j, kt = gpairs[gi]
        if j == 0 and kt == 0:
            assert not nxt_box[0]
            nxt_box[0] = head_inproj_tasks(h + 1, hsets[(h + 1) % 2]) if h + 1 < NH else []
            if h == NH - 1:
                wo_box[0] = load_wo()
        if gi + 2 < ngp:
            issue_S(gi + 2)
        issue_rest(gi)
    while evac_pending:
        evac_pending.pop(0)()
    flush_rope()
    while release_pending:
        release_pending.pop(0)()
    while followups_a:
        followups_a.pop(0)()
    flush_followups()
    while deferred:
        r_ = deferred.pop(0)[1]()
        if r_ is not None:
            r_()
    wo_bufs = wo_box[0]

    if STOP == 5:
        P.final_wait('sp', [k for k in P.semcount if k not in P.eng_sem.values()]); P.final_wait('pool', list(P.eng_sem.values())); P.emit(); P.close(); return nc
    s_cc2 = newsem("cc")
    cc_op("mixA2_in", "mixA2_out", mixA2_in, mixA2_out, s_cc2)

    odyn = Arena(dyn_lo + 20 * 1024, head_end)
    xrt = [sb("xrt%d" % i, [1024], F32, odyn) for i in range(2)]
    ostg = [sb("ostg%d" % i, [1024], F32, odyn) for i in range(3)]
    fgb = sb("fgb", [1024], F32, odyn)
    junk2 = sb("junk2", [1024], BF16, odyn)
    ssqh = [sb("ssq%d" % i, [8], F32, odyn) for i in range(4)]
    ssgh = [sb("ssg%d" % i, [2, 8], F32, odyn) for i in range(4)]
    sdoh = [sb("sdo%d" % i, [8], F32, odyn) for i in range(4)]
    rsoh = [sb("rso%d" % i, [8], F32, odyn) for i in range(4)]
    GB = [0, 8, 14, 16]
    grp_of = {}
    NGRP = len(GB) - 1
    for g_ in range(NGRP):
        for T_ in range(GB[g_], GB[g_ + 1]):
            grp_of[T_] = (g_, T_ - GB[g_])
    for g_ in range(NGRP):
        MEMSET("pool", ssqh[g_][:], 1.0)
    ybuf = Buf(P, "ybuf", [16, 1024], F32, A_off)
    for c in range(11, 15):
        mixc[c] = Buf(P, "mixc%d" % c, [S], BF16, B_off + (c - 11) * 4096)
    mixc[15] = sb("mixc15", [S], BF16, odyn)
    s_fg = newsem("c")
    DMA("sp", fgb[:], fg_d[0:1, :].partition_broadcast(128), s_fg, writes=[fgb[:]])
    s_xr = [newsem("xr"), newsem("xr")]
    yps = [(accb[0], accb[1]), (rps, mps)]
    s_mx = [newsem("mx"), newsem("mx")]

    s_mxa = [newsem("mx"), newsem("mx")]

    def load_mix(half, chunks, sems):
        tk = slice(half * 1024, (half + 1) * 1024)
        for c in chunks:
            src, key, r0 = chunk_src(c)
            DMA("sp", mixc[c][:, tk], src.ap()[r0:r0 + 128, half * 1024:(half + 1) * 1024], sems[half],
                reads=[Dram(key)], writes=[mixc[c][:, tk]])

    s_out = [newsem("o") for _ in range(3)]

    ex_sems = {}
    ex_done = set()

    def exchange(half):
        s1, s2, s3 = newsem("c"), newsem("cc"), newsem("c")
        ex_sems[half] = s3
        DMA("act", ss_in_h[half][:, :], ssqh[half][:], s1, reads=[ssqh[half][:]], writes=[Dram("ss_in%d" % half)])
        cc_op("ss_in%d" % half, "ss_out%d" % half, ss_in_h[half], ss_out_h[half], s2)

    def exchange_finish(half):
        if half in ex_done:
            return
        ex_done.add(half)
        DMA("sp" if half == NGRP - 1 else "pool", ssgh[half][:], ss_out_h[half].ap().rearrange("(s p) t -> p s t", p=128),
            ex_sems[half], reads=[Dram("ss_out%d" % half)], writes=[ssgh[half][:]])
        TT("pool", sdoh[half][:], ssgh[half][:, 0, :], ssgh[half][:, 1, :], ALU.add)
        TS("pool", sdoh[half][:], sdoh[half][:], 1.0 / D, EPS, ALU.mult, ALU.add)
        TT("pool", rsoh[half][:], sdoh[half][:], mhalf16[:, 0:8], ALU.pow)

    def finalize_tile(T):
        og = ostg[T % 3]
        g_, k_ = grp_of[T]
        exchange_finish(g_)
        STT(og[:], ybuf[:, T, :], rsoh[g_][:, k_:k_ + 1], fgb[:], ALU.mult, ALU.mult)
        DMA("sp", y_d[T * 128:(T + 1) * 128, :], og[:], s_out[T % 3], reads=[og[:]])

    def load_xres(T):
        DMA("sp", xrt[T % 2][:], xres_d[T * 128:(T + 1) * 128, :], s_xr[T % 2], writes=[xrt[T % 2][:]])

    load_mix(0, [11, 12, 13], s_mx)
    load_xres(0)
    load_xres(1)
    load_mix(1, [11, 12, 13], s_mx)
    load_mix(0, [14, 15], s_mxa)
    load_mix(1, [14, 15], s_mxa)

    def partial_group(yp, T, nb, chunks, first, last):
        items = [(mixc[c][:, T * 128:(T + 1) * 128], wo_bufs[c][:, nb * 512:(nb + 1) * 512]) for c in chunks]
        reads = [v for it in items for v in it]

        def fn(e):
            ins = None
            for k, (l, r) in enumerate(items):
                ins = e.matmul(yp.t[:, 0:512], l.ap, r.ap, start=(first and k == 0), stop=(last and k == len(items) - 1))
            return ins
        P.add("pe", fn, reads=reads, writes=[yp[:]])

    pre_banks = [accb[0], accb[1], rps, mps, Sps[0], Sps[1], Ops, Zps]
    NPRE = 4
    for T in range(NPRE):
        for nb in range(2):
            partial_group(pre_banks[2 * T + nb], T, nb, list(range(14)), True, False)
    fin_queue = []
    for tb in range(4):
        for tt in range(4):
            T = 4 * tb + tt
            xr_t = xrt[T % 2]
            for nb in range(2):
                if T < NPRE:
                    yp = pre_banks[2 * T + nb]
                    partial_group(yp, T, nb, [14, 15], False, True)
                else:
                    yp = yps[T % 2][nb]
                    MMG(yp[:], [(mixc[c][:, T * 128:(T + 1) * 128], wo_bufs[c][:, nb * 512:(nb + 1) * 512]) for c in range(16)])
                TT("dve", ybuf[:, T, nb * 512:(nb + 1) * 512], yp[:], xr_t[:, nb * 512:(nb + 1) * 512], ALU.add)
            g_, k_ = grp_of[T]
            ACT(junk2[:], ybuf[:, T, :], AF.Square, accum=ssqh[g_][:, k_:k_ + 1])
            if T + 2 < 16:
                load_xres(T + 2)
            for _ in range(2):
                if fin_queue and fin_queue[0][0] <= T:
                    finalize_tile(fin_queue.pop(0)[1])
            if T + 1 == GB[g_ + 1]:
                exchange(g_)
                for T2 in range(GB[g_], GB[g_ + 1]):
                    fin_queue.append((T + 3, T2))
    while fin_queue:
        finalize_tile(fin_queue.pop(0)[1])
    P.final_wait("sp", s_out)
    P.emit()
    P.close()
    return nc


def _host_layout(inputs):
    f32 = np.float32
    x = np.asarray(inputs["x"], f32)
    pos = np.asarray(inputs["positions"], np.int32)
    w_in = np.asarray(inputs["w_in"], f32)[0]
    w_out = np.asarray(inputs["w_out"], f32)[0]
    ng = np.asarray(inputs["norm_gain"], f32)[0]
    fg = np.asarray(inputs["final_gain"], f32)
    lamv = np.stack([np.asarray(inputs[k], f32)[0] for k in ("lambda_q1", "lambda_k1", "lambda_q2", "lambda_k2")], 0)
    subln = np.asarray(inputs["subln_gain"], f32)[0]
    conv_w = np.asarray(inputs["conv_w"], f32)[0]
    conv_b = np.asarray(inputs["conv_b"], f32)[0]
    w_a = np.asarray(inputs["w_a"], f32)[0]
    w_x = np.asarray(inputs["w_x"], f32)[0]
    b_a = np.asarray(inputs["b_a"], f32)[0]
    b_x = np.asarray(inputs["b_x"], f32)[0]
    lru = np.asarray(inputs["lru_lambda"], f32)[0]

    p = np.arange(128)
    d = p % 64
    invf = (10000.0 ** (-(2.0 * (d % 32)) / 64.0) / (2.0 * np.pi)).astype(f32)
    sgn = np.where(d < 32, -1.0, 1.0).astype(f32)
    ident = np.eye(128, dtype=f32)
    partner = (p // 64) * 64 + (d + 32) % 64
    perm = np.zeros((128, 128), f32)
    perm[partner, p] = 1.0
    ones = np.ones((128, 128), f32)
    xq = np.arange(256)[None, :]
    mA = np.where(xq >= p[:, None], 0.0, -30000.0).astype(f32)
    mB = np.where(xq >= (128 + p)[:, None], 0.0, -30000.0).astype(f32)
    cbf = np.concatenate([ident, perm, ones, mA, mA, mB, mB], 1).astype(ml_dtypes.bfloat16)

    maps = []
    for core in range(8):
        b, hh = core // 2, core % 2
        m = {}
        m["x"] = np.ascontiguousarray(x[b])
        m["xres"] = np.ascontiguousarray(x[b][:, hh * 1024:(hh + 1) * 1024])
        m["pos"] = np.ascontiguousarray(pos[b][None, :])
        m["ng"] = np.ascontiguousarray(ng[None, :])
        vcols = 2048 + (4 * hh) * 128 + np.arange(512)
        wv = w_in[:, vcols].reshape(16, 128, 512).transpose(1, 0, 2)
        m["wv"] = np.ascontiguousarray(wv.reshape(128, 16 * 512))
        cts = []
        for n in range(4):
            cts.append(4096 + (4 * hh + n) * 128)
            cts.append(5120 + (4 * hh + n) * 128)
        for h in range(4):
            cts.append(0 + (4 * hh + h) * 128)
            cts.append(1024 + (4 * hh + h) * 128)
            cts.append(3072 + (4 * hh + h) * 128)
        wct = np.stack([w_in[:, c0:c0 + 128].reshape(16, 128, 128).transpose(1, 0, 2).reshape(128, 2048) for c0 in cts], 0)
        m["wct"] = np.ascontiguousarray(wct)
        rows = []
        for s in range(2):
            for n in range(4):
                rows.append(1024 + (4 * s + n) * 128)
        for s in range(2):
            for h in range(3):
                rows.append((4 * s + h) * 128)
        for s in range(2):
            rows.append((4 * s + 3) * 128)
        wo = np.stack([w_out[r0:r0 + 128, hh * 1024:(hh + 1) * 1024] for r0 in rows], 0)
        m["wo"] = np.ascontiguousarray(wo)
        m["lamv"] = np.ascontiguousarray(lamv.reshape(1, 256))
        vecs = np.zeros((128, 40), f32)
        vecs[:, 0] = invf
        vecs[:, 1] = sgn
        vecs[:, 2] = subln
        for n in range(4):
            ch = (4 * hh + n) * 128 + p
            for j in range(4):
                vecs[:, 4 + n * 4 + j] = conv_w[j, ch]
            vecs[:, 20 + n] = conv_b[ch]
            vecs[:, 24 + n] = b_a[ch]
            vecs[:, 28 + n] = b_x[ch]
            vecs[:, 32 + n] = lru[ch]
        m["vecs"] = vecs
        m["fg"] = np.ascontiguousarray(fg[None, hh * 1024:(hh + 1) * 1024])
        m["wa"] = np.ascontiguousarray(w_a[4 * hh:4 * hh + 4].transpose(1, 0, 2).reshape(128, 512))
        m["wx"] = np.ascontiguousarray(w_x[4 * hh:4 * hh + 4].transpose(1, 0, 2).reshape(128, 512))
        m["cbf"] = cbf
        maps.append(m)
    return maps


_NC_CACHE = {}


def kernel(**inputs):
    maps = _host_layout(inputs)
    if "nc" not in _NC_CACHE:
        _NC_CACHE["nc"] = build_program()
    nc = _NC_CACHE["nc"]
    res = run_bass_kernel_spmd(nc, maps, core_ids=list(range(8)))
    out = np.zeros((4, S, D), np.float32)
    for core in range(8):
        b, hh = core // 2, core % 2
        out[b][:, hh * 1024:(hh + 1) * 1024] = np.asarray(res.results[core]["y"], np.float32)
    return out
```

```python
import numpy as np
import concourse.bass as bass
import concourse.mybir as mybir

F32 = mybir.dt.float32
BF16 = mybir.dt.bfloat16
I32 = mybir.dt.int32
AF = mybir.ActivationFunctionType
ALU = mybir.AluOpType
AX = mybir.AxisListType

PAGE = 256
SB_BYTES = 224 * 1024
SB_BASE = 17 * 1024
PS_BYTES = 16 * 1024
DT_SIZE = {F32: 4, BF16: 2, I32: 4}


class View:
    __slots__ = ("ap", "space", "q0", "q1", "b0", "b1")

    def __init__(self, ap, space, q0, q1, b0, b1):
        self.ap, self.space, self.q0, self.q1, self.b0, self.b1 = ap, space, q0, q1, b0, b1


class Buf:
    def __init__(self, prog, name, shape, dtype, offset, space="sb", parts=128):
        self.prog, self.name, self.shape, self.dtype = prog, name, list(shape), dtype
        self.offset, self.space, self.parts = offset, space, parts
        self.esz = DT_SIZE[dtype]
        n = 1
        for s in shape:
            n *= s
        self.nbytes = n * self.esz
        if space == "sb":
            assert offset + self.nbytes <= SB_BYTES, (name, offset, self.nbytes)
            self.t = prog.nc.alloc_sbuf_tensor_at(name, [parts] + list(shape), dtype, offset=offset)
        else:
            self.t = None
        strides = []
        acc = 1
        for s in reversed(shape):
            strides.append(acc)
            acc *= s
        self.strides = list(reversed(strides))

    def __getitem__(self, idx):
        if not isinstance(idx, tuple):
            idx = (idx,)
        idx = list(idx)
        while len(idx) < 1 + len(self.shape):
            idx.append(slice(None))
        ps = idx[0]
        if isinstance(ps, slice):
            p0 = ps.start or 0
            p1 = self.parts if ps.stop is None else ps.stop
        else:
            p0, p1 = ps, ps + 1
        lo = 0
        hi = 0
        for k, (ix, n, st) in enumerate(zip(idx[1:], self.shape, self.strides)):
            if isinstance(ix, slice):
                a = ix.start or 0
                b = n if ix.stop is None else ix.stop
                step = ix.step or 1
                last = a + ((b - a - 1) // step) * step
            else:
                a, last = ix, ix
            lo += a * st
            hi += last * st
        b0 = self.offset + lo * self.esz
        b1 = self.offset + (hi + 1) * self.esz
        ap = self.t[tuple(idx)]
        return View(ap, self.space, p0 // 32, (p1 + 31) // 32, b0, b1)


class Dram:
    def __init__(self, key):
        self.space, self.key = "dram", key


class Op:
    __slots__ = ("eng", "fn", "waits", "sem", "inc", "val")


class Prog:
    ENGS = ("pe", "act", "dve", "pool", "sp")

    def __init__(self, nc):
        self.nc = nc
        self.ops = {e: [] for e in self.ENGS}
        self.sems = {}
        self.semcount = {}
        self.eng_sem = {}
        self.waited = {e: {} for e in self.ENGS}
        self.lastw = {}
        self.lastr = {}
        self.dram_w = {}
        self.dram_r = {}
        self._ctx = []
        for e in ("pe", "act", "dve", "pool"):
            self.eng_sem[e] = self.new_sem("e_" + e)
        self.psum_banks = []
        self.n_ops = 0

    def new_sem(self, name):
        g = self.nc.semaphore(name)
        s = g.__enter__()
        self._ctx.append(g)
        self.sems[name] = s
        self.semcount[name] = 0
        return name

    def psum(self, name, bank, shape=(512,), dtype=F32):
        b = Buf(self, name, shape, dtype, bank * 2048, space="ps")
        g = self.nc.psum_tensor(name, [128] + list(shape), dtype)
        b.t = g.__enter__()
        self._ctx.append(g)
        return b

    def _pages(self, v):
        pg = 2048 if v.space == "ps" else PAGE
        p0 = v.b0 // pg
        p1 = (v.b1 + pg - 1) // pg
        for q in range(v.q0, v.q1):
            for p in range(p0, p1):
                yield (v.space, q, p)

    def _need(self, need, d):
        for s, val in d.items():
            if need.get(s, 0) < val:
                need[s] = val

    def add(self, eng, fn, reads=(), writes=(), dma_sem=None, cc=False):
        op = Op()
        op.eng, op.fn = eng, fn
        if cc:
            op.sem, op.inc = dma_sem, None
            self.semcount[op.sem] += 1
        elif dma_sem is not None:
            op.sem, op.inc = dma_sem, 16
            self.semcount[op.sem] += 16
        else:
            op.sem, op.inc = self.eng_sem[eng], 1
            self.semcount[op.sem] += 1
        op.val = self.semcount[op.sem]
        ps_reads = [v for v in reads if v.space == "ps"]
        if ps_reads:
            reads = [v for v in reads if v.space != "ps"]
            writes = list(writes) + ps_reads
        need = {}
        for v in reads:
            if v.space == "dram":
                self._need(need, self.dram_w.get(v.key, {}))
            else:
                for k in self._pages(v):
                    d = self.lastw.get(k)
                    if d:
                        self._need(need, d)
        for v in writes:
            if v.space == "dram":
                self._need(need, self.dram_w.get(v.key, {}))
                self._need(need, self.dram_r.get(v.key, {}))
            else:
                for k in self._pages(v):
                    d = self.lastw.get(k)
                    if d:
                        self._need(need, d)
                    d = self.lastr.get(k)
                    if d:
                        self._need(need, d)
        w = self.waited[eng]
        waits = []
        engsems = set(self.eng_sem.values())
        for s, val in need.items():
            if eng == "pe" and s == self.eng_sem["pe"]:
                continue
            if s not in engsems:
                val = self.semcount[s] - (op.inc if (s == op.sem and op.inc) else 0)
                if s == op.sem and op.inc is None:
                    val = self.semcount[s] - 1
            if w.get(s, 0) >= val:
                continue
            w[s] = val
            waits.append((s, val))
        op.waits = waits
        tok = {op.sem: op.val}
        for v in writes:
            if v.space == "dram":
                self.dram_w[v.key] = dict(tok)
                self.dram_r[v.key] = {}
            else:
                for k in self._pages(v):
                    self.lastw[k] = tok
                    self.lastr[k] = {}
        for v in reads:
            if v.space == "dram":
                d = self.dram_r.setdefault(v.key, {})
                if d.get(op.sem, 0) < op.val:
                    d[op.sem] = op.val
            else:
                for k in self._pages(v):
                    d = self.lastr.get(k)
                    if d is None or len(d) == 0:
                        d = {}
                        self.lastr[k] = d
                    if d.get(op.sem, 0) < op.val:
                        d[op.sem] = op.val
        self.ops[eng].append(op)
        self.n_ops += 1
        return op

    def final_wait(self, eng, sems):
        waits = [(s, self.semcount[s]) for s in sems if self.semcount[s] > 0]
        op = Op()
        op.eng, op.fn, op.waits, op.sem, op.inc, op.val = eng, None, waits, None, 0, 0
        self.ops[eng].append(op)

    def emit(self):
        nc = self.nc
        engobj = {"pe": "tensor", "act": "scalar", "dve": "vector", "pool": "gpsimd", "sp": "sync"}
        with nc.Block() as block:
            for e in self.ENGS:
                ops = self.ops[e]
                if not ops:
                    continue

                def body(eng, ops=ops):
                    for op in ops:
                        for s, val in op.waits:
                            eng.wait_ge(self.sems[s], val)
                        if op.fn is None:
                            continue
                        ins = op.fn(eng)
                        if op.inc is None:
                            ins.then_inc(self.sems[op.sem])
                        else:
                            ins.then_inc(self.sems[op.sem], op.inc)

                getattr(block, engobj[e])(body)

    def close(self):
        for g in reversed(self._ctx):
            g.__exit__(None, None, None)


import ml_dtypes
from concourse.bass_utils import run_bass_kernel_spmd

S = 2048
D = 2048
NH = 4
NRB = 4
EPS = 1e-6
LAM_INIT = 0.2
TWO_PI_SCALE = 6.28318
WARM_DUP = 0
PAIRS = [[0, 1], [2, 3], [4, 5], [6, 7]]


class Arena:
    def __init__(self, lo, hi):
        self.lo, self.hi, self.off = lo, hi, lo

    def take(self, nbytes):
        nbytes = (nbytes + 255) // 256 * 256
        o = self.off
        self.off += nbytes
        assert self.off <= self.hi, ("arena overflow", self.off, self.hi)
        return o


def build_program(PAIRS=PAIRS, STOP=99):
    nc = bass.Bass("TRN2", target_bir_lowering=False)
    dt = nc.dram_tensor
    x_d = dt("x", [S, D], F32, kind="ExternalInput")
    xres_d = dt("xres", [S, 1024], F32, kind="ExternalInput")
    pos_d = dt("pos", [1, S], I32, kind="ExternalInput")
    ng_d = dt("ng", [1, D], F32, kind="ExternalInput")
    wv_d = dt("wv", [128, 16 * 512], F32, kind="ExternalInput")
    wct_d = dt("wct", [20, 128, 16 * 128], F32, kind="ExternalInput")
    wo_d = dt("wo", [16, 128, 1024], F32, kind="ExternalInput")
    lam_d = dt("lamv", [1, 256], F32, kind="ExternalInput")
    vec_d = dt("vecs", [128, 40], F32, kind="ExternalInput")
    fg_d = dt("fg", [1, 1024], F32, kind="ExternalInput")
    wa_d = dt("wa", [128, 4 * 128], F32, kind="ExternalInput")
    wx_d = dt("wx", [128, 4 * 128], F32, kind="ExternalInput")
    cbf_d = dt("cbf", [128, 384 + 1024], BF16, kind="ExternalInput")
    y_d = dt("y", [S, 1024], F32, kind="ExternalOutput")
    mixR_in = dt("mixR_in", [4 * 128, S], BF16)
    mixR_out = dt("mixR_out", [8 * 128, S], BF16)
    mixA1_in = dt("mixA1_in", [3 * 128, S], BF16)
    mixA1_out = dt("mixA1_out", [6 * 128, S], BF16)
    mixA2_in = dt("mixA2_in", [1 * 128, S], BF16)
    mixA2_out = dt("mixA2_out", [2 * 128, S], BF16)
    ss_in_h = [dt("ss_in%d" % i, [128, 8], F32) for i in range(4)]
    ss_out_h = [dt("ss_out%d" % i, [256, 8], F32) for i in range(4)]

    P = Prog(nc)
    ar = Arena(SB_BASE, SB_BYTES)

    def sb(name, shape, dtype, arena=None):
        a = arena or ar
        n = int(np.prod(shape)) * DT_SIZE[dtype]
        return Buf(P, name, shape, dtype, a.take(n))

    def _v(x):
        return x.ap if isinstance(x, View) else x

    def ACT(out, in_, func, bias=None, scale=None, accum=None, extra_reads=()):
        reads = [in_] + list(extra_reads)
        writes = [out]
        kw = {}
        if bias is not None:
            kw["bias"] = _v(bias)
            if isinstance(bias, View):
                reads.append(bias)
        if scale is not None:
            kw["scale"] = _v(scale)
            if isinstance(scale, View):
                reads.append(scale)
        if accum is not None:
            kw["accum_out"] = accum.ap
            writes.append(accum)
        P.add("act", lambda e: e.activation(out.ap, in_.ap, func, **kw), reads=reads, writes=writes)

    def TS(eng, out, in0, s1, s2, op0, op1=None):
        reads = [in0] + [s for s in (s1, s2) if isinstance(s, View)]
        if op1 is None:
            P.add(eng, lambda e: e.tensor_scalar(out.ap, in0.ap, _v(s1), None, op0), reads=reads, writes=[out])
        else:
            P.add(eng, lambda e: e.tensor_scalar(out.ap, in0.ap, _v(s1), _v(s2), op0, op1), reads=reads, writes=[out])

    def TT(eng, out, in0, in1, op):
        P.add(eng, lambda e: e.tensor_tensor(out.ap, in0.ap, in1.ap, op), reads=[in0, in1], writes=[out])

    def STT(out, in0, sc, in1, op0, op1):
        reads = [in0, in1] + ([sc] if isinstance(sc, View) else [])
        P.add("dve", lambda e: e.scalar_tensor_tensor(out.ap, in0.ap, _v(sc), in1.ap, op0, op1), reads=reads, writes=[out])

    def RECIP(out, in_):
        P.add("dve", lambda e: e.reciprocal(out.ap, in_.ap), reads=[in_], writes=[out])

    def COPY(eng, out, in_):
        if eng == "act":
            P.add("act", lambda e: e.activation(out.ap, in_.ap, AF.Copy), reads=[in_], writes=[out])
        else:
            P.add(eng, lambda e: e.tensor_copy(out.ap, in_.ap), reads=[in_], writes=[out])

    def MEMSET(eng, out, val):
        P.add(eng, lambda e: e.memset(out.ap, val), writes=[out])

    def MMG(out, items, extra_reads=()):
        reads = list(extra_reads)
        for (l, r) in items:
            reads.append(l)
            reads.append(r)
        n = len(items)

        def fn(e):
            ins = None
            for k, (l, r) in enumerate(items):
                ins = e.matmul(out.ap, l.ap, r.ap, start=(k == 0), stop=(k == n - 1))
            return ins
        P.add("pe", fn, reads=reads, writes=[out])

    def MM(out, l, r, start=True, stop=True):
        P.add("pe", lambda e: e.matmul(out.ap, l.ap, r.ap, start=start, stop=stop), reads=[l, r], writes=[out])

    def DMA(eng, out, in_, sem, reads=(), writes=()):
        P.add(eng, lambda e: e.dma_start(out=_v(out), in_=_v(in_)), reads=list(reads), writes=list(writes), dma_sem=sem)

    sem_id = [0]

    def newsem(prefix="s"):
        sem_id[0] += 1
        return P.new_sem("%s%d" % (prefix, sem_id[0]))

    hT = sb("hT", [16, S], BF16)
    A_off = hT.offset
    Vt = sb("Vt", [16, 512], BF16)
    B_off = Vt.offset
    cosT = sb("cosT", [S], F32)
    sinT = sb("sinT", [S], F32)
    C_off = cosT.offset
    wbuf = [sb("wbuf%d" % i, [16, 128], BF16) for i in range(3)]
    W_off = wbuf[0].offset
    cbf = sb("cbf", [384 + 1024], BF16)
    ident = cbf[:, 0:128]
    perm = cbf[:, 128:256]
    onesb = cbf[:, 256:384]
    maskA = cbf[:, 384:896]
    maskB = cbf[:, 896:1408]
    ones32 = sb("ones32", [128], F32)
    vecs = sb("vecs", [40], F32)
    lamb = sb("lamb", [256], F32)
    sm = sb("sm", [64], F32)
    ssA = sb("ssA", [16], F32)
    sdA = sb("sdA", [16], F32)
    rsA = sb("rsA", [16], F32)
    sp8 = sb("sp8", [4], F32)
    mhalf16 = sb("mhalf16", [16], F32)
    sp16 = sb("sp16", [4], F32)
    wab = sb("wab", [512], BF16)
    wxb = sb("wxb", [512], BF16)
    eps_t = sm[:, 0:1]
    one_t = sm[:, 1:2]
    nlam = sm[:, 2:3]
    gsub = sm[:, 3:4]
    dyn_lo = ar.off
    dyn_hi = SB_BYTES - 256

    accb = [P.psum("acc0", 0), P.psum("acc1", 1)]
    rps = P.psum("rps", 2)
    mps = P.psum("mps", 3)
    Sps = [P.psum("S0", 4), P.psum("S1", 5), rps]
    Ops = P.psum("Ops", 6)
    Zps = P.psum("Zps", 7)

    def ps_bf16_view(buf):
        ap = buf.t[:, 0:512].bitcast(BF16).rearrange("p (c k) -> p c k", c=8)
        return View(ap, "ps", 0, 4, buf.offset, buf.offset + 2048)

    s_c = newsem("c")
    DMA("sp", cbf[:], cbf_d[:, :], s_c, writes=[cbf[:]])
    s_c2 = newsem("c")
    DMA("sp", vecs[:], vec_d[:, :], s_c2, writes=[vecs[:]])
    s_c3 = newsem("c")
    DMA("sp", lamb[:], lam_d[0:1, :].partition_broadcast(128), s_c3, writes=[lamb[:]])
    s_c4 = newsem("c")
    DMA("pool", wab[:], wa_d[:, :], s_c4, writes=[wab[:]])
    s_c5 = newsem("c")
    DMA("pool", wxb[:], wx_d[:, :], s_c5, writes=[wxb[:]])
    MEMSET("dve", ones32[:], 1.0)
    MEMSET("dve", eps_t, EPS)
    MEMSET("dve", one_t, 1.0)
    lt = sb("lt", [2, 64], F32)
    TT("dve", lt[:, 0, :], lamb[:, 0:64], lamb[:, 64:128], ALU.mult)
    TT("dve", lt[:, 1, :], lamb[:, 128:192], lamb[:, 192:256], ALU.mult)
    P.add("dve", lambda e: e.tensor_reduce(sm[:, 4:6].ap, lt[:].ap, AX.X, ALU.add), reads=[lt[:]], writes=[sm[:, 4:6]])
    ACT(sm[:, 6:8], sm[:, 4:6], AF.Exp)
    TT("dve", sm[:, 8:9], sm[:, 7:8], sm[:, 6:7], ALU.subtract)
    TS("dve", nlam, sm[:, 8:9], -LAM_INIT, None, ALU.add)
    TS("dve", gsub, vecs[:, 2:3], 1.0 - LAM_INIT, None, ALU.mult)
    ACT(sm[:, 12:16], vecs[:, 32:36], AF.Exp, scale=-1.0)
    ACT(sm[:, 16:20], sm[:, 12:16], AF.Ln, bias=one_t, scale=1.0)
    TS("dve", sp8[:], sm[:, 16:20], -8.0, None, ALU.mult)
    TS("dve", sp16[:], sm[:, 16:20], -16.0, None, ALU.mult)

    if STOP == 0:
        P.final_wait('sp', [k for k in P.semcount if k not in P.eng_sem.values()]); P.final_wait('pool', list(P.eng_sem.values())); P.emit(); P.close(); return nc
    dyn = Arena(dyn_lo, dyn_hi)
    wvb = sb("wvb", [16, 512], BF16, dyn)
    s_wv = newsem("w")

    def load_wv():
        for q4 in range(4):
            DMA("pool", wvb[:, q4 * 4:(q4 + 1) * 4, :], wv_d[:, q4 * 2048:(q4 + 1) * 2048].rearrange("p (c k) -> p c k", k=512), s_wv,
                writes=[wvb[:, q4 * 4:(q4 + 1) * 4, :]])
    xst = [sb("xst%d" % i, [D], F32, dyn) for i in range(4)]
    junk = sb("junk", [D], BF16, dyn)
    xn = [sb("xn%d" % i, [D], BF16, dyn) for i in range(3)]
    gbc = sb("gbc", [D], F32, dyn)
    mhalf = sm[:, 10:11]
    MEMSET("pool", mhalf, -0.5)
    MEMSET("pool", mhalf16[:], -0.5)
    s_g = newsem("c")
    DMA("sp", gbc[:], ng_d[0:1, :].partition_broadcast(128), s_g, writes=[gbc[:]])
    s_x = [newsem("x") for _ in range(4)]

    def stage1(t):
        xt = xst[t % 4]
        DMA("sp", xt[:], x_d[t * 128:(t + 1) * 128, :], s_x[t % 4], writes=[xt[:]])
        ACT(junk[:], xt[:], AF.Square, accum=ssA[:, t:t + 1])
        TS("pool", sdA[:, t:t + 1], ssA[:, t:t + 1], 1.0 / D, EPS, ALU.mult, ALU.add)
        TT("pool", rsA[:, t:t + 1], sdA[:, t:t + 1], mhalf, ALU.pow)
        STT(xn[t % 3][:], xt[:], rsA[:, t:t + 1], gbc[:], ALU.mult, ALU.mult)

    def stage2(t):
        for half in range(2):
            pb = accb[half]
            pv = ps_bf16_view(pb)
            xnb = xn[t % 3]

            def fn(e, pb=pb, xnb=xnb, half=half):
                ins = None
                bf = pb.t[:, 0:512].bitcast(BF16)
                for c in range(8):
                    cc = half * 8 + c
                    ins = e.transpose(bf[:, c * 128:(c + 1) * 128], xnb.t[:, cc * 128:(cc + 1) * 128], ident.ap)
                return ins
            P.add("pe", fn, reads=[xnb[:], ident], writes=[pv])
            dst = hT[:, half * 8:(half + 1) * 8, t * 128:(t + 1) * 128]
            if half == 0:
                P.add("act", lambda e, pv=pv, dst=dst: e.activation(dst.ap, pv.ap, AF.Copy), reads=[pv], writes=[dst])
            else:
                P.add("dve", lambda e, pv=pv, dst=dst: e.tensor_copy(dst.ap, pv.ap), reads=[pv], writes=[dst])

    vbanks = [rps, mps]

    def stage3(t):
        acc = vbanks[t % 2]
        MMG(acc[:], [(hT[:, c, t * 128:(t + 1) * 128], wvb[:, c, :]) for c in range(16)])
        COPY("act", Vt[:, t, :], acc[:])

    for t in range(16 + 2):
        if t < 16:
            stage1(t)
        if t == 4:
            load_wv()
        if 0 <= t - 2 < 16:
            stage2(t - 2)
    dyn = Arena(dyn_lo + 16384, dyn_hi)
    posi = sb("posi", [S], I32, dyn)
    tt_ = sb("ttab", [S], F32, dyn)
    ti_ = sb("titab", [S], I32, dyn)
    tf_ = sb("tftab", [S], F32, dyn)
    tf2_ = sb("tf2tab", [S], F32, dyn)
    s_p = newsem("c")
    DMA("sp", posi[:], pos_d[0:1, :].partition_broadcast(128), s_p, writes=[posi[:]])
    COPY("dve", tf_[:], posi[:])
    TS("dve", tt_[:], tf_[:], vecs[:, 0:1], None, ALU.mult)
    COPY("dve", ti_[:], tt_[:])
    COPY("dve", tf_[:], ti_[:])
    TT("dve", tf_[:], tt_[:], tf_[:], ALU.subtract)
    TS("dve", tt_[:], tt_[:], 0.25, None, ALU.add)
    COPY("dve", ti_[:], tt_[:])
    COPY("dve", tf2_[:], ti_[:])
    TT("dve", tf2_[:], tt_[:], tf2_[:], ALU.subtract)
    for t in range(16):
        stage3(t)
        if t == 6:
            ACT(sinT[:], tf_[:], AF.Sin, scale=TWO_PI_SCALE)
        if t == 11:
            ACT(cosT[:], tf2_[:], AF.Sin, scale=TWO_PI_SCALE)
    TS("dve", sinT[:], sinT[:], vecs[:, 1:2], None, ALU.mult)

    if STOP == 2:
        P.final_wait('sp', [k for k in P.semcount if k not in P.eng_sem.values()]); P.final_wait('pool', list(P.eng_sem.values())); P.emit(); P.close(); return nc
    s_w = [newsem("w") for _ in range(3)]
    wstate = {"next": 0}

    def w_prefetch(upto):
        while wstate["next"] <= min(upto, 19):
            ct = wstate["next"]
            DMA("pool", wbuf[ct % 3][:], wct_d[ct, :, :].rearrange("p (c k) -> p c k", k=128), s_w[ct % 3], writes=[wbuf[ct % 3][:]])
            wstate["next"] += 1

    gcount = [0]

    acc_banks = [accb[0], accb[1]]

    def inproj_group(ct, tb):
        acc = acc_banks[gcount[0] % len(acc_banks)]
        gcount[0] += 1
        wb = wbuf[ct % 3]
        MMG(acc[:], [(wb[:, c, :], hT[:, c, tb * 512:(tb + 1) * 512]) for c in range(16)])
        return acc

    dyn = Arena(dyn_lo, dyn_hi)
    hsets = []
    for i in range(2):
        qz = sb("qz%d" % i, [8, 2, 256], BF16, dyn)
        kT = sb("kT%d" % i, [S], BF16, dyn)
        gs = sb("gs%d" % i, [S], BF16, dyn)
        mixh = sb("mixh%d" % i, [S], BF16, dyn)
        hsets.append((qz, kT, gs, mixh))
    ET = [sb("ET%d" % i, [512], BF16, dyn) for i in range(3)]
    rsb = sb("rsb", [512], F32, dyn)
    tbuf = sb("tbuf", [512], F32, dyn)
    obufs = [sb("obuf%d" % i, [256], F32, dyn) for i in range(2)]
    osqs = [sb("osq%d" % i, [256], F32, dyn) for i in range(2)]
    sdbs = [sb("sdb%d" % i, [256], F32, dyn) for i in range(2)]
    rstds = [sb("rstd%d" % i, [256], F32, dyn) for i in range(2)]
    onbs = [sb("onb%d" % i, [256], F32, dyn) for i in range(2)]
    head_end = dyn.off
    tail = Arena(dyn_hi - 10 * 1024, dyn_hi)
    qbf = [sb("qbf%d" % i, [512], BF16, tail) for i in range(2)]
    t1b = [sb("t1b%d" % i, [2, 256], F32, tail) for i in range(2)]
    t2b = [sb("t2b%d" % i, [2, 256], F32, tail) for i in range(2)]
    TAIL_LO = dyn_hi - 10 * 1024
    WO_END = head_end + 4096
    rdyn = Arena(dyn_lo, TAIL_LO)
    xpads = [sb("xpad%d" % i, [S + 3], F32, rdyn) for i in range(2)]
    cvbs = [sb("cvb%d" % i, [S], BF16, rdyn) for i in range(2)]
    gsrs = [sb("gsr%d" % i, [S], BF16, rdyn) for i in range(3)]
    cvs = [sb("cv%d" % i, [S], F32, rdyn) for i in range(2)]
    rgf = sb("rgf", [S], F32, rdyn)
    igf = sb("igf", [S], F32, rdyn)
    afull = sb("afull", [S], F32, rdyn)
    mixr = sb("mixr", [S], BF16, rdyn)


    rope_cnt = [0]

    rope_pending = []

    def flush_rope():
        while rope_pending:
            rope_pending.pop(0)()

    def evac_rope(acc, tb, kind, hs):
        i = rope_cnt[0] % 2
        rope_cnt[0] += 1
        tok = slice(tb * 512, (tb + 1) * 512)
        COPY("act", qbf[i][:], acc[:])
        t1v = t1b[i][:]
        t2v = t2b[i][:]
        P.add("dve", lambda e, a=acc, o=t1b[i]: e.tensor_tensor(o.t[:, :, :].rearrange("p a b -> p (a b)"), a.t[:, 0:512], cosT.t[:, tok], ALU.mult),
              reads=[acc[:], cosT[:, tok]], writes=[t1v])

        def part2():
            MM(mps[:], perm, qbf[i][:])
            P.add("dve", lambda e, o=t2b[i]: e.tensor_tensor(o.t[:, :, :].rearrange("p a b -> p (a b)"), mps.t[:, 0:512], sinT.t[:, tok], ALU.mult),
                  reads=[mps[:], sinT[:, tok]], writes=[t2v])
            if kind == "q":
                qz = hs[0]
                for m in range(2):
                    ps = slice(64 * m, 64 * (m + 1))
                    TT("pool", qz[ps, 2 * tb:2 * tb + 2, m, :], t1b[i][ps, :, :], t2b[i][ps, :, :], ALU.add)
            else:
                kT = hs[1]
                P.add("pool", lambda e, o=kT, a=t1b[i], b=t2b[i]: e.tensor_tensor(
                    o.t[:, tok], a.t[:, :, :].rearrange("p a b -> p (a b)"), b.t[:, :, :].rearrange("p a b -> p (a b)"), ALU.add),
                    reads=[t1v, t2v], writes=[kT[:, tok]])
        rope_pending.append(part2)

    def head_inproj_tasks(h, hs):
        tasks = []
        base = 8 + 3 * h
        for j, kind in enumerate(("q", "k", "g")):
            ct = base + j
            for tb in range(4):
                def task(ct=ct, tb=tb, kind=kind):
                    if tb == 0:
                        w_prefetch(ct + 2)
                    acc = inproj_group(ct, tb)
                    flush_rope()

                    def evac(acc=acc):
                        if kind == "g":
                            ACT(hs[2][:, tb * 512:(tb + 1) * 512], acc[:], AF.Silu)
                        else:
                            evac_rope(acc, tb, kind, hs)
                    return evac
                tasks.append(task)
        return tasks

    s_mr = newsem("m")
    w_prefetch(1)
    h0_tasks = None
    for i in range(2):
        MEMSET("dve", xpads[i][:, 0:3], 0.0)
    gps = [rps, mps]

    def rec_inproj(n):
        xpad, gsr = xpads[n % 2], gsrs[n % 3]
        ctx_, ctg = 2 * n, 2 * n + 1
        w_prefetch(ctx_ + 2)
        for tb in range(4):
            acc = inproj_group(ctx_, tb)
            COPY("act", xpad[:, 3 + tb * 512:3 + (tb + 1) * 512], acc[:])
        w_prefetch(ctg + 2)
        for tb in range(4):
            acc = inproj_group(ctg, tb)
            ACT(gsr[:, tb * 512:(tb + 1) * 512], acc[:], AF.Silu)

    def rec_conv(n):
        xpad, cv, cvb = xpads[n % 2], cvs[n % 2], cvbs[n % 2]
        cw = lambda j: vecs[:, 4 + n * 4 + j:5 + n * 4 + j]
        TS("dve", cv[:], xpad[:, 0:S], cw(0), vecs[:, 20 + n:21 + n], ALU.mult, ALU.add)
        for j in range(1, 4):
            STT(cv[:], xpad[:, j:j + S], cw(j), cv[:], ALU.mult, ALU.add)
        COPY("dve", cvb[:], cv[:])

    def rec_gates(n):
        cvb = cvbs[n % 2]
        for tb in range(4):
            tok = slice(tb * 512, (tb + 1) * 512)
            MM(gps[0][:], wab[:, n * 128:(n + 1) * 128], cvb[:, tok])
            ACT(rgf[:, tok], gps[0][:], AF.Sigmoid, bias=vecs[:, 24 + n:25 + n], scale=1.0)
            MM(gps[1][:], wxb[:, n * 128:(n + 1) * 128], cvb[:, tok])
            ACT(igf[:, tok], gps[1][:], AF.Sigmoid, bias=vecs[:, 28 + n:29 + n], scale=1.0)

    def rec_post(n, part=None):
        cv, gsr = cvs[n % 2], gsrs[n % 3]
        if part in (None, "a"):
            TT("dve", igf[:], igf[:], cv[:], ALU.mult)
            ACT(afull[:], rgf[:], AF.Exp, scale=sp8[:, n:n + 1])
            ACT(rgf[:], rgf[:], AF.Exp, scale=sp16[:, n:n + 1])
            ACT(rgf[:], rgf[:], AF.Ln, bias=one_t, scale=-1.0)
            ACT(rgf[:], rgf[:], AF.Exp, scale=0.5)
        if part == "a":
            return
        TT("dve", igf[:], igf[:], rgf[:], ALU.mult)
        P.add("dve", lambda e: e.tensor_tensor_scan(rgf[:].ap, afull[:].ap, igf[:].ap, 0.0, ALU.mult, ALU.add),
              reads=[afull[:], igf[:]], writes=[rgf[:]])
        TT("dve", mixr[:], rgf[:], gsr[:], ALU.mult)
        DMA("sp", mixR_in[n * 128:(n + 1) * 128, :], mixr[:], s_mr, reads=[mixr[:]], writes=[Dram("mixR_in")])

    rec_inproj(0)
    rec_conv(0)
    rec_inproj(1)
    for n in range(NRB):
        rec_gates(n)
        if n + 1 < NRB:
            rec_conv(n + 1)
        if n + 2 < NRB:
            rec_post(n)
            rec_inproj(n + 2)
            continue
        if n == NRB - 2:
            rec_post(n)
            h0_tasks = head_inproj_tasks(0, hsets[0])
            MEMSET("pool", hsets[0][0][:], 0.0)
            acc_banks.extend([Sps[0], Sps[1], Ops, Zps])
            for tk in h0_tasks[0:4]:
                tk()()
            continue
        if n == NRB - 1:
            rec_post(n, "a")
            for tk in h0_tasks[4:12]:
                tk()()
            flush_rope()
            rec_post(n, "b")
            del acc_banks[2:]
            gcount[0] = 0
            continue
        rec_post(n)

    if STOP == 3:
        P.final_wait('sp', [k for k in P.semcount if k not in P.eng_sem.values()]); P.final_wait('pool', list(P.eng_sem.values())); P.emit(); P.close(); return nc
    s_cc1 = newsem("cc")

    def cc_op(kind_in, kind_out, tin, tout, sem):
        op = P.add("pool", lambda e: e.collective_compute("AllGather", ALU.bypass, replica_groups=PAIRS,
                                                          ins=[tin.ap().opt()], outs=[tout.ap().opt()]),
                   reads=[Dram(kind_in)], writes=[Dram(kind_out)], dma_sem=sem, cc=True)
    cc_op("mixR_in", "mixR_out", mixR_in, mixR_out, s_cc1)

    if STOP == 4:
        P.final_wait('sp', [k for k in P.semcount if k not in P.eng_sem.values()]); P.final_wait('pool', list(P.eng_sem.values())); P.emit(); P.close(); return nc
    MEMSET("dve", hsets[1][0][:], 0.0)
    if STOP == 401:
        P.final_wait('sp', [k for k in P.semcount if k not in P.eng_sem.values()]); P.final_wait('pool', list(P.eng_sem.values())); P.emit(); P.close(); return nc
    if STOP == 41:
        P.final_wait('sp', [k for k in P.semcount if k not in P.eng_sem.values()]); P.final_wait('pool', list(P.eng_sem.values())); P.emit(); P.close(); return nc
    epi_cnt = [0]
    s_ma = newsem("m")
    s_ma2 = newsem("m")
    s_cc2a = newsem("cc")
    pair_cnt = [0]
    srcs = [(mixR_out, "mixR_out", 0, 4, 0), (mixR_out, "mixR_out", 1, 4, 4), (mixA1_out, "mixA1_out", 0, 3, 8),
            (mixA1_out, "mixA1_out", 1, 3, 11), (mixA2_out, "mixA2_out", 0, 1, 14), (mixA2_out, "mixA2_out", 1, 1, 15)]

    def chunk_src(c):
        for (src, key, slot, nch, c0) in srcs:
            if c0 <= c < c0 + nch:
                return src, key, (slot * nch + (c - c0)) * 128
    mixc = [None] * 16
    early_offs = [WO_END + i * 4096 for i in range(4)] + [TAIL_LO + i * 4096 for i in range(2)] + [dyn_lo + i * 4096 for i in range(5)]
    assert WO_END + 4 * 4096 <= TAIL_LO
    for c in range(11):
        mixc[c] = Buf(P, "mixc%d" % c, [S], BF16, early_offs[c])
    s_mxe = newsem("mx")

    def load_mix_early():
        for c in range(11):
            src, key, r0 = chunk_src(c)
            DMA("sp", mixc[c][:], src.ap()[r0:r0 + 128, :], s_mxe, reads=[Dram(key)], writes=[mixc[c][:]])

    def load_wo():
        wo_bufs = []
        carena = Arena(C_off, C_off + 16384)
        for c in range(8):
            wo_bufs.append(sb("wo%d" % c, [1024], BF16, carena))
        warena = Arena(W_off, W_off + 12288)
        for c in range(8, 14):
            wo_bufs.append(sb("wo%d" % c, [1024], BF16, warena))
        odyn = Arena(head_end, head_end + 4096)
        for c in range(14, 16):
            wo_bufs.append(sb("wo%d" % c, [1024], BF16, odyn))
        s_wo = newsem("w")
        for c in range(16):
            DMA("pool", wo_bufs[c][:], wo_d[c, :, :], s_wo, writes=[wo_bufs[c][:]])
        return wo_bufs

    wo_box = [None]
    deferred = []
    evac_pending = []
    nxt_box = [[]]
    gpairs = [(h, j, kt) for h in range(NH) for j in range(8) for kt in range(2 * (j + 1))]
    ngp = len(gpairs)

    def issue_S(gi):
        h, j, kt = gpairs[gi]
        qz, kT, gs, mixh = hsets[h % 2]
        Sb = Sps[gi % 3]
        qv = qz[:, j, :, :]
        ktv = kT[:, kt * 128:(kt + 1) * 128]
        if kt == 2 * j + 1:
            def fnb(e, Sb=Sb, qz=qz, kT=kT, j=j, kt=kt):
                ins = None
                for m in range(2):
                    o2 = Sb.t[:, m * 256 + 128:m * 256 + 256]
                    e.matmul(o2, kT.t[:, kt * 128:(kt + 1) * 128], qz.t[:, j, m, 128:256], start=True, stop=False)
                    ins = e.matmul(o2, ident.ap, cbf.t[:, 896 + m * 256 + 128:896 + m * 256 + 256], start=False, stop=True)
                return ins
            P.add("pe", fnb, reads=[ktv, qv, ident, maskB], writes=[Sb[:]])
            return
        diag = (kt == 2 * j)

        def fn(e, Sb=Sb, qz=qz, kT=kT, j=j, kt=kt, diag=diag):
            ins = e.matmul(Sb.t[:, 0:512], kT.t[:, kt * 128:(kt + 1) * 128],
                           qz.t[:, j, :, :].rearrange("p a b -> p (a b)"), start=True, stop=(not diag))
            if diag:
                ins = e.matmul(Sb.t[:, 0:512], ident.ap, maskA.ap, start=False, stop=True)
            return ins
        P.add("pe", fn, reads=[ktv, qv] + ([ident, maskA] if diag else []), writes=[Sb[:]])

    followups = []
    followups_a = []

    def flush_followups():
        while followups:
            followups.pop(0)()

    def pop_tasks(j):
        n = (1, 1, 1, 1, 1, 1, 2, 4)[j]
        nxt = nxt_box[0]
        for _ in range(n):
            if nxt:
                while len(evac_pending) >= 2:
                    evac_pending.pop(0)()
                evac_pending.append(nxt.pop(0)())

    def finish_head(h):
        mixh = hsets[h % 2][3]
        if h < 3:
            DMA("sp", mixA1_in[h * 128:(h + 1) * 128, :], mixh[:], s_ma, reads=[mixh[:]], writes=[Dram("mixA1_in")])
        else:
            DMA("sp", mixA2_in[0:128, :], mixh[:], s_ma2, reads=[mixh[:]], writes=[Dram("mixA2_in")])
        if h == 2:
            cc_op("mixA1_in", "mixA1_out", mixA1_in, mixA1_out, s_cc2a)
            load_mix_early()

    def issue_rest(gi):
        h, j, kt = gpairs[gi]
        qz, kT, gs, mixh = hsets[h % 2]
        nkt = 2 * (j + 1)
        Sb = Sps[gi % 3]
        E = ET[gi % 3]
        if kt == 2 * j + 1:
            def v3(t):
                return t[:, 0:512].rearrange("p (m x) -> p m x", m=2)[:, :, 128:256]
            P.add("act", lambda e, Sb=Sb, E=E: e.activation(v3(E.t), v3(Sb.t), AF.Exp, scale=0.125),
                  reads=[Sb[:]], writes=[E[:]])
            Vv = Vt[:, kt, h * 128:(h + 1) * 128]

            def c2(t, m):
                return t[:, m * 256 + 128:m * 256 + 256]

            def fpv(e, E=E, Vv=Vv):
                e.matmul(c2(Ops.t, 0), Vv.ap, c2(E.t, 0), start=False, stop=False)
                return e.matmul(c2(Ops.t, 1), Vv.ap, c2(E.t, 1), start=False, stop=True)

            def fz(e, E=E):
                e.matmul(c2(Zps.t, 0), onesb.ap, c2(E.t, 0), start=False, stop=False)
                return e.matmul(c2(Zps.t, 1), onesb.ap, c2(E.t, 1), start=False, stop=True)
            P.add("pe", fpv, reads=[Vv, E[:]], writes=[Ops[:]])
            P.add("pe", fz, reads=[onesb, E[:]], writes=[Zps[:]])
        else:
            ACT(E[:], Sb[:], AF.Exp, scale=0.125)
            MM(Ops[:], Vt[:, kt, h * 128:(h + 1) * 128], E[:], start=(kt == 0), stop=(kt == nkt - 1))
            MM(Zps[:], onesb, E[:], start=(kt == 0), stop=(kt == nkt - 1))
        if kt == nkt - 1:
            ACT(rsb[:], Zps[:], AF.Ln)
            COPY("dve", tbuf[:], Ops[:])
        flush_followups()
        while followups_a:
            followups_a.pop(0)()
        while evac_pending:
            evac_pending.pop(0)()
        flush_rope()
        for d_ in list(deferred):
            d_[0] -= 1
            if d_[0] <= 0 and deferred and deferred[0] is d_:
                r_ = deferred.pop(0)[1]()
                if r_ is not None:
                    followups.append(r_)
        if kt != nkt - 1:
            return
        qs = slice(j * 256, (j + 1) * 256)
        eb = epi_cnt[0] % 2
        epi_cnt[0] += 1
        while len(deferred) > 0 and deferred[0][2] <= epi_cnt[0] - 2:
            r_ = deferred.pop(0)[1]()
            if r_ is not None:
                followups.append(r_)
        if followups and epi_cnt[0] >= 2:
            flush_followups()
        obuf, osq, sdb, rstd, onb = obufs[eb], osqs[eb], sdbs[eb], rstds[eb], onbs[eb]
        pop_tasks(j)
        def stage_b(qs=qs, gs=gs, mixh=mixh, obuf=obuf, osq=osq, sdb=sdb, rstd=rstd, onb=onb):
            MM(mps[:, 0:256], ones32[:], osq[:])

            def stage_c():
                ACT(sdb[:], mps[:, 0:256], AF.Ln, bias=eps_t, scale=1.0 / 128.0)
                ACT(rstd[:], sdb[:], AF.Exp, scale=-0.5)
                STT(onb[:], obuf[:], gsub, rstd[:], ALU.mult, ALU.mult)
                TT("pool", mixh[:, qs], onb[:], gs[:, qs], ALU.mult)
            return stage_c

        def stage_a(j=j, h=h, obuf=obuf, osq=osq, stage_b=stage_b, cnt=epi_cnt[0]):
            ACT(rsb[:], rsb[:], AF.Exp, scale=-1.0)
            TT("dve", tbuf[:], tbuf[:], rsb[:], ALU.mult)
            STT(obuf[:], tbuf[:, 256:512], nlam, tbuf[:, 0:256], ALU.mult, ALU.add)
            TT("pool", osq[:], obuf[:], obuf[:], ALU.mult)
            deferred.append([13, stage_b, cnt])
            if j == 7:
                deferred.append([14, lambda h=h: (flush_followups(), finish_head(h), None)[2], cnt])
        followups_a.append(stage_a)

    issue_S(0)
    issue_S(1)
    for gi in range(ngp):
        h, j, kt = gpairs[gi]
        if j == 0 and kt == 0:
            assert not nxt_box[0]
            nxt_box[0] = head_inproj_tasks(h + 1, hsets[(h + 1) % 2]) if h + 1 < NH else []
            if h == NH - 1:
                wo_box[0] = load_wo()
        if gi + 2 < ngp:
            issue_S(gi + 2)
        issue_rest(gi)
    while evac_pending:
        evac_pending.pop(0)()
    flush_rope()
    while followups_a:
        followups_a.pop(0)()
    flush_followups()
    while deferred:
        r_ = deferred.pop(0)[1]()
        if r_ is not None:
            r_()
    wo_bufs = wo_box[0]

    if STOP == 5:
        P.final_wait('sp', [k for k in P.semcount if k not in P.eng_sem.values()]); P.final_wait('pool', list(P.eng_sem.values())); P.emit(); P.close(); return nc
    s_cc2 = newsem("cc")
    cc_op("mixA2_in", "mixA2_out", mixA2_in, mixA2_out, s_cc2)

    odyn = Arena(dyn_lo + 20 * 1024, head_end)
    xrt = [sb("xrt%d" % i, [1024], F32, odyn) for i in range(2)]
    ostg = [sb("ostg%d" % i, [1024], F32, odyn) for i in range(3)]
    fgb = sb("fgb", [1024], F32, odyn)
    junk2 = sb("junk2", [1024], BF16, odyn)
    ssqh = [sb("ssq%d" % i, [8], F32, odyn) for i in range(4)]
    ssgh = [sb("ssg%d" % i, [2, 8], F32, odyn) for i in range(4)]
    sdoh = [sb("sdo%d" % i, [8], F32, odyn) for i in range(4)]
    rsoh = [sb("rso%d" % i, [8], F32, odyn) for i in range(4)]
    GB = [0, 8, 14, 16]
    grp_of = {}
    NGRP = len(GB) - 1
    for g_ in range(NGRP):
        for T_ in range(GB[g_], GB[g_ + 1]):
            grp_of[T_] = (g_, T_ - GB[g_])
    for g_ in range(NGRP):
        MEMSET("pool", ssqh[g_][:], 1.0)
    ybuf = Buf(P, "ybuf", [16, 1024], F32, A_off)
    for c in range(11, 15):
        mixc[c] = Buf(P, "mixc%d" % c, [S], BF16, B_off + (c - 11) * 4096)
    mixc[15] = sb("mixc15", [S], BF16, odyn)
    s_fg = newsem("c")
    DMA("sp", fgb[:], fg_d[0:1, :].partition_broadcast(128), s_fg, writes=[fgb[:]])
    s_xr = [newsem("xr"), newsem("xr")]
    yps = [(accb[0], accb[1]), (rps, mps)]
    s_mx = [newsem("mx"), newsem("mx")]

    s_mxa = [newsem("mx"), newsem("mx")]

    def load_mix(half, chunks, sems):
        tk = slice(half * 1024, (half + 1) * 1024)
        for c in chunks:
            src, key, r0 = chunk_src(c)
            DMA("sp", mixc[c][:, tk], src.ap()[r0:r0 + 128, half * 1024:(half + 1) * 1024], sems[half],
                reads=[Dram(key)], writes=[mixc[c][:, tk]])

    s_out = [newsem("o") for _ in range(3)]

    ex_sems = {}
    ex_done = set()

    def exchange(half):
        s1, s2, s3 = newsem("c"), newsem("cc"), newsem("c")
        ex_sems[half] = s3
        DMA("act", ss_in_h[half][:, :], ssqh[half][:], s1, reads=[ssqh[half][:]], writes=[Dram("ss_in%d" % half)])
        cc_op("ss_in%d" % half, "ss_out%d" % half, ss_in_h[half], ss_out_h[half], s2)

    def exchange_finish(half):
        if half in ex_done:
            return
        ex_done.add(half)
        DMA("sp" if half == NGRP - 1 else "pool", ssgh[half][:], ss_out_h[half].ap().rearrange("(s p) t -> p s t", p=128),
            ex_sems[half], reads=[Dram("ss_out%d" % half)], writes=[ssgh[half][:]])
        TT("pool", sdoh[half][:], ssgh[half][:, 0, :], ssgh[half][:, 1, :], ALU.add)
        TS("pool", sdoh[half][:], sdoh[half][:], 1.0 / D, EPS, ALU.mult, ALU.add)
        TT("pool", rsoh[half][:], sdoh[half][:], mhalf16[:, 0:8], ALU.pow)

    def finalize_tile(T):
        og = ostg[T % 3]
        g_, k_ = grp_of[T]
        exchange_finish(g_)
        STT(og[:], ybuf[:, T, :], rsoh[g_][:, k_:k_ + 1], fgb[:], ALU.mult, ALU.mult)
        DMA("sp", y_d[T * 128:(T + 1) * 128, :], og[:], s_out[T % 3], reads=[og[:]])

    def load_xres(T):
        DMA("sp", xrt[T % 2][:], xres_d[T * 128:(T + 1) * 128, :], s_xr[T % 2], writes=[xrt[T % 2][:]])

    load_mix(0, [11, 12, 13], s_mx)
    load_xres(0)
    load_xres(1)
    load_mix(1, [11, 12, 13], s_mx)
    load_mix(0, [14, 15], s_mxa)
    load_mix(1, [14, 15], s_mxa)

    def partial_group(yp, T, nb, chunks, first, last):
        items = [(mixc[c][:, T * 128:(T + 1) * 128], wo_bufs[c][:, nb * 512:(nb + 1) * 512]) for c in chunks]
        reads = [v for it in items for v in it]

        def fn(e):
            ins = None
            for k, (l, r) in enumerate(items):
                ins = e.matmul(yp.t[:, 0:512], l.ap, r.ap, start=(first and k == 0), stop=(last and k == len(items) - 1))
            return ins
        P.add("pe", fn, reads=reads, writes=[yp[:]])

    pre_banks = [accb[0], accb[1], rps, mps, Sps[0], Sps[1], Ops, Zps]
    NPRE = 4
    for T in range(NPRE):
        for nb in range(2):
            partial_group(pre_banks[2 * T + nb], T, nb, list(range(14)), True, False)
    fin_queue = []
    for tb in range(4):
        for tt in range(4):
            T = 4 * tb + tt
            xr_t = xrt[T % 2]
            for nb in range(2):
                if T < NPRE:
                    yp = pre_banks[2 * T + nb]
                    partial_group(yp, T, nb, [14, 15], False, True)
                else:
                    yp = yps[T % 2][nb]
                    MMG(yp[:], [(mixc[c][:, T * 128:(T + 1) * 128], wo_bufs[c][:, nb * 512:(nb + 1) * 512]) for c in range(16)])
                TT("dve", ybuf[:, T, nb * 512:(nb + 1) * 512], yp[:], xr_t[:, nb * 512:(nb + 1) * 512], ALU.add)
            g_, k_ = grp_of[T]
            ACT(junk2[:], ybuf[:, T, :], AF.Square, accum=ssqh[g_][:, k_:k_ + 1])
            if T + 2 < 16:
                load_xres(T + 2)
            for _ in range(2):
                if fin_queue and fin_queue[0][0] <= T:
                    finalize_tile(fin_queue.pop(0)[1])
            if T + 1 == GB[g_ + 1]:
                exchange(g_)
                for T2 in range(GB[g_], GB[g_ + 1]):
                    fin_queue.append((T + 3, T2))
    while fin_queue:
        finalize_tile(fin_queue.pop(0)[1])
    P.final_wait("sp", s_out)
    P.emit()
    P.close()
    return nc


def _host_layout(inputs):
    f32 = np.float32
    x = np.asarray(inputs["x"], f32)
    pos = np.asarray(inputs["positions"], np.int32)
    w_in = np.asarray(inputs["w_in"], f32)[0]
    w_out = np.asarray(inputs["w_out"], f32)[0]
    ng = np.asarray(inputs["norm_gain"], f32)[0]
    fg = np.asarray(inputs["final_gain"], f32)
    lamv = np.stack([np.asarray(inputs[k], f32)[0] for k in ("lambda_q1", "lambda_k1", "lambda_q2", "lambda_k2")], 0)
    subln = np.asarray(inputs["subln_gain"], f32)[0]
    conv_w = np.asarray(inputs["conv_w"], f32)[0]
    conv_b = np.asarray(inputs["conv_b"], f32)[0]
    w_a = np.asarray(inputs["w_a"], f32)[0]
    w_x = np.asarray(inputs["w_x"], f32)[0]
    b_a = np.asarray(inputs["b_a"], f32)[0]
    b_x = np.asarray(inputs["b_x"], f32)[0]
    lru = np.asarray(inputs["lru_lambda"], f32)[0]

    p = np.arange(128)
    d = p % 64
    invf = (10000.0 ** (-(2.0 * (d % 32)) / 64.0) / (2.0 * np.pi)).astype(f32)
    sgn = np.where(d < 32, -1.0, 1.0).astype(f32)
    ident = np.eye(128, dtype=f32)
    partner = (p // 64) * 64 + (d + 32) % 64
    perm = np.zeros((128, 128), f32)
    perm[partner, p] = 1.0
    ones = np.ones((128, 128), f32)
    xq = np.arange(256)[None, :]
    mA = np.where(xq >= p[:, None], 0.0, -30000.0).astype(f32)
    mB = np.where(xq >= (128 + p)[:, None], 0.0, -30000.0).astype(f32)
    cbf = np.concatenate([ident, perm, ones, mA, mA, mB, mB], 1).astype(ml_dtypes.bfloat16)

    maps = []
    for core in range(8):
        b, hh = core // 2, core % 2
        m = {}
        m["x"] = np.ascontiguousarray(x[b])
        m["xres"] = np.ascontiguousarray(x[b][:, hh * 1024:(hh + 1) * 1024])
        m["pos"] = np.ascontiguousarray(pos[b][None, :])
        m["ng"] = np.ascontiguousarray(ng[None, :])
        vcols = 2048 + (4 * hh) * 128 + np.arange(512)
        wv = w_in[:, vcols].reshape(16, 128, 512).transpose(1, 0, 2)
        m["wv"] = np.ascontiguousarray(wv.reshape(128, 16 * 512))
        cts = []
        for n in range(4):
            cts.append(4096 + (4 * hh + n) * 128)
            cts.append(5120 + (4 * hh + n) * 128)
        for h in range(4):
            cts.append(0 + (4 * hh + h) * 128)
            cts.append(1024 + (4 * hh + h) * 128)
            cts.append(3072 + (4 * hh + h) * 128)
        wct = np.stack([w_in[:, c0:c0 + 128].reshape(16, 128, 128).transpose(1, 0, 2).reshape(128, 2048) for c0 in cts], 0)
        m["wct"] = np.ascontiguousarray(wct)
        rows = []
        for s in range(2):
            for n in range(4):
                rows.append(1024 + (4 * s + n) * 128)
        for s in range(2):
            for h in range(3):
                rows.append((4 * s + h) * 128)
        for s in range(2):
            rows.append((4 * s + 3) * 128)
        wo = np.stack([w_out[r0:r0 + 128, hh * 1024:(hh + 1) * 1024] for r0 in rows], 0)
        m["wo"] = np.ascontiguousarray(wo)
        m["lamv"] = np.ascontiguousarray(lamv.reshape(1, 256))
        vecs = np.zeros((128, 40), f32)
        vecs[:, 0] = invf
        vecs[:, 1] = sgn
        vecs[:, 2] = subln
        for n in range(4):
            ch = (4 * hh + n) * 128 + p
            for j in range(4):
                vecs[:, 4 + n * 4 + j] = conv_w[j, ch]
            vecs[:, 20 + n] = conv_b[ch]
            vecs[:, 24 + n] = b_a[ch]
            vecs[:, 28 + n] = b_x[ch]
            vecs[:, 32 + n] = lru[ch]
        m["vecs"] = vecs
        m["fg"] = np.ascontiguousarray(fg[None, hh * 1024:(hh + 1) * 1024])
        m["wa"] = np.ascontiguousarray(w_a[4 * hh:4 * hh + 4].transpose(1, 0, 2).reshape(128, 512))
        m["wx"] = np.ascontiguousarray(w_x[4 * hh:4 * hh + 4].transpose(1, 0, 2).reshape(128, 512))
        m["cbf"] = cbf
        maps.append(m)
    return maps


_NC_CACHE = {}


def kernel(**inputs):
    maps = _host_layout(inputs)
    if "nc" not in _NC_CACHE:
        _NC_CACHE["nc"] = build_program()
    nc = _NC_CACHE["nc"]
    res = run_bass_kernel_spmd(nc, maps, core_ids=list(range(8)))
    out = np.zeros((4, S, D), np.float32)
    for core in range(8):
        b, hh = core // 2, core % 2
        out[b][:, hh * 1024:(hh + 1) * 1024] = np.asarray(res.results[core]["y"], np.float32)
    return out
```

```python
import numpy as np
import concourse.bass as bass
import concourse.mybir as mybir

F32 = mybir.dt.float32
BF16 = mybir.dt.bfloat16
I32 = mybir.dt.int32
AF = mybir.ActivationFunctionType
ALU = mybir.AluOpType
AX = mybir.AxisListType

PAGE = 256
SB_BYTES = 224 * 1024
SB_BASE = 17 * 1024
PS_BYTES = 16 * 1024
DT_SIZE = {F32: 4, BF16: 2, I32: 4}


class View:
    __slots__ = ("ap", "space", "q0", "q1", "b0", "b1")

    def __init__(self, ap, space, q0, q1, b0, b1):
        self.ap, self.space, self.q0, self.q1, self.b0, self.b1 = ap, space, q0, q1, b0, b1


class Buf:
    def __init__(self, prog, name, shape, dtype, offset, space="sb", parts=128):
        self.prog, self.name, self.shape, self.dtype = prog, name, list(shape), dtype
        self.offset, self.space, self.parts = offset, space, parts
        self.esz = DT_SIZE[dtype]
        n = 1
        for s in shape:
            n *= s
        self.nbytes = n * self.esz
        if space == "sb":
            assert offset + self.nbytes <= SB_BYTES, (name, offset, self.nbytes)
            self.t = prog.nc.alloc_sbuf_tensor_at(name, [parts] + list(shape), dtype, offset=offset)
        else:
            self.t = None
        strides = []
        acc = 1
        for s in reversed(shape):
            strides.append(acc)
            acc *= s
        self.strides = list(reversed(strides))

    def __getitem__(self, idx):
        if not isinstance(idx, tuple):
            idx = (idx,)
        idx = list(idx)
        while len(idx) < 1 + len(self.shape):
            idx.append(slice(None))
        ps = idx[0]
        if isinstance(ps, slice):
            p0 = ps.start or 0
            p1 = self.parts if ps.stop is None else ps.stop
        else:
            p0, p1 = ps, ps + 1
        lo = 0
        hi = 0
        for k, (ix, n, st) in enumerate(zip(idx[1:], self.shape, self.strides)):
            if isinstance(ix, slice):
                a = ix.start or 0
                b = n if ix.stop is None else ix.stop
                step = ix.step or 1
                last = a + ((b - a - 1) // step) * step
            else:
                a, last = ix, ix
            lo += a * st
            hi += last * st
        b0 = self.offset + lo * self.esz
        b1 = self.offset + (hi + 1) * self.esz
        ap = self.t[tuple(idx)]
        return View(ap, self.space, p0 // 32, (p1 + 31) // 32, b0, b1)


class Dram:
    def __init__(self, key):
        self.space, self.key = "dram", key


class Op:
    __slots__ = ("eng", "fn", "waits", "sem", "inc", "val")


class Prog:
    ENGS = ("pe", "act", "dve", "pool", "sp")

    def __init__(self, nc):
        self.nc = nc
        self.ops = {e: [] for e in self.ENGS}
        self.sems = {}
        self.semcount = {}
        self.eng_sem = {}
        self.waited = {e: {} for e in self.ENGS}
        self.lastw = {}
        self.lastr = {}
        self.dram_w = {}
        self.dram_r = {}
        self._ctx = []
        for e in ("pe", "act", "dve", "pool"):
            self.eng_sem[e] = self.new_sem("e_" + e)
        self.psum_banks = []
        self.n_ops = 0

    def new_sem(self, name):
        g = self.nc.semaphore(name)
        s = g.__enter__()
        self._ctx.append(g)
        self.sems[name] = s
        self.semcount[name] = 0
        return name

    def psum(self, name, bank, shape=(512,), dtype=F32):
        b = Buf(self, name, shape, dtype, bank * 2048, space="ps")
        g = self.nc.psum_tensor(name, [128] + list(shape), dtype)
        b.t = g.__enter__()
        self._ctx.append(g)
        return b

    def _pages(self, v):
        pg = 2048 if v.space == "ps" else PAGE
        p0 = v.b0 // pg
        p1 = (v.b1 + pg - 1) // pg
        for q in range(v.q0, v.q1):
            for p in range(p0, p1):
                yield (v.space, q, p)

    def _need(self, need, d):
        for s, val in d.items():
            if need.get(s, 0) < val:
                need[s] = val

    def add(self, eng, fn, reads=(), writes=(), dma_sem=None, cc=False):
        op = Op()
        op.eng, op.fn = eng, fn
        if cc:
            op.sem, op.inc = dma_sem, None
            self.semcount[op.sem] += 1
        elif dma_sem is not None:
            op.sem, op.inc = dma_sem, 16
            self.semcount[op.sem] += 16
        else:
            op.sem, op.inc = self.eng_sem[eng], 1
            self.semcount[op.sem] += 1
        op.val = self.semcount[op.sem]
        ps_reads = [v for v in reads if v.space == "ps"]
        if ps_reads:
            reads = [v for v in reads if v.space != "ps"]
            writes = list(writes) + ps_reads
        need = {}
        for v in reads:
            if v.space == "dram":
                self._need(need, self.dram_w.get(v.key, {}))
            else:
                for k in self._pages(v):
                    d = self.lastw.get(k)
                    if d:
                        self._need(need, d)
        for v in writes:
            if v.space == "dram":
                self._need(need, self.dram_w.get(v.key, {}))
                self._need(need, self.dram_r.get(v.key, {}))
            else:
                for k in self._pages(v):
                    d = self.lastw.get(k)
                    if d:
                        self._need(need, d)
                    d = self.lastr.get(k)
                    if d:
                        self._need(need, d)
        w = self.waited[eng]
        waits = []
        engsems = set(self.eng_sem.values())
        for s, val in need.items():
            if eng == "pe" and s == self.eng_sem["pe"]:
                continue
            if s not in engsems:
                val = self.semcount[s] - (op.inc if (s == op.sem and op.inc) else 0)
                if s == op.sem and op.inc is None:
                    val = self.semcount[s] - 1
            if w.get(s, 0) >= val:
                continue
            w[s] = val
            waits.append((s, val))
        op.waits = waits
        tok = {op.sem: op.val}
        for v in writes:
            if v.space == "dram":
                self.dram_w[v.key] = dict(tok)
                self.dram_r[v.key] = {}
            else:
                for k in self._pages(v):
                    self.lastw[k] = tok
                    self.lastr[k] = {}
        for v in reads:
            if v.space == "dram":
                d = self.dram_r.setdefault(v.key, {})
                if d.get(op.sem, 0) < op.val:
                    d[op.sem] = op.val
            else:
                for k in self._pages(v):
                    d = self.lastr.get(k)
                    if d is None or len(d) == 0:
                        d = {}
                        self.lastr[k] = d
                    if d.get(op.sem, 0) < op.val:
                        d[op.sem] = op.val
        self.ops[eng].append(op)
        self.n_ops += 1
        return op

    def final_wait(self, eng, sems):
        waits = [(s, self.semcount[s]) for s in sems if self.semcount[s] > 0]
        op = Op()
        op.eng, op.fn, op.waits, op.sem, op.inc, op.val = eng, None, waits, None, 0, 0
        self.ops[eng].append(op)

    def emit(self):
        nc = self.nc
        engobj = {"pe": "tensor", "act": "scalar", "dve": "vector", "pool": "gpsimd", "sp": "sync"}
        with nc.Block() as block:
            for e in self.ENGS:
                ops = self.ops[e]
                if not ops:
                    continue

                def body(eng, ops=ops):
                    for op in ops:
                        for s, val in op.waits:
                            eng.wait_ge(self.sems[s], val)
                        if op.fn is None:
                            continue
                        ins = op.fn(eng)
                        if op.inc is None:
                            ins.then_inc(self.sems[op.sem])
                        else:
                            ins.then_inc(self.sems[op.sem], op.inc)

                getattr(block, engobj[e])(body)

    def close(self):
        for g in reversed(self._ctx):
            g.__exit__(None, None, None)


import ml_dtypes
from concourse.bass_utils import run_bass_kernel_spmd

S = 2048
D = 2048
NH = 4
NRB = 4
EPS = 1e-6
LAM_INIT = 0.2
TWO_PI_SCALE = 6.28318
WARM_DUP = 0
PAIRS = [[0, 1], [2, 3], [4, 5], [6, 7]]


class Arena:
    def __init__(self, lo, hi):
        self.lo, self.hi, self.off = lo, hi, lo

    def take(self, nbytes):
        nbytes = (nbytes + 255) // 256 * 256
        o = self.off
        self.off += nbytes
        assert self.off <= self.hi, ("arena overflow", self.off, self.hi)
        return o


def build_program(PAIRS=PAIRS, STOP=99):
    nc = bass.Bass("TRN2", target_bir_lowering=False)
    dt = nc.dram_tensor
    x_d = dt("x", [S, D], F32, kind="ExternalInput")
    xres_d = dt("xres", [S, 1024], F32, kind="ExternalInput")
    pos_d = dt("pos", [1, S], I32, kind="ExternalInput")
    ng_d = dt("ng", [1, D], F32, kind="ExternalInput")
    wv_d = dt("wv", [128, 16 * 512], F32, kind="ExternalInput")
    wct_d = dt("wct", [20, 128, 16 * 128], F32, kind="ExternalInput")
    wo_d = dt("wo", [16, 128, 1024], F32, kind="ExternalInput")
    lam_d = dt("lamv", [1, 256], F32, kind="ExternalInput")
    vec_d = dt("vecs", [128, 40], F32, kind="ExternalInput")
    fg_d = dt("fg", [1, 1024], F32, kind="ExternalInput")
    wa_d = dt("wa", [128, 4 * 128], F32, kind="ExternalInput")
    wx_d = dt("wx", [128, 4 * 128], F32, kind="ExternalInput")
    cbf_d = dt("cbf", [128, 384 + 1024], BF16, kind="ExternalInput")
    y_d = dt("y", [S, 1024], F32, kind="ExternalOutput")
    mixR_in = dt("mixR_in", [4 * 128, S], BF16)
    mixR_out = dt("mixR_out", [8 * 128, S], BF16)
    mixA1_in = dt("mixA1_in", [3 * 128, S], BF16)
    mixA1_out = dt("mixA1_out", [6 * 128, S], BF16)
    mixA2_in = dt("mixA2_in", [1 * 128, S], BF16)
    mixA2_out = dt("mixA2_out", [2 * 128, S], BF16)
    ss_in_h = [dt("ss_in%d" % i, [128, 8], F32) for i in range(4)]
    ss_out_h = [dt("ss_out%d" % i, [256, 8], F32) for i in range(4)]

    P = Prog(nc)
    ar = Arena(SB_BASE, SB_BYTES)

    def sb(name, shape, dtype, arena=None):
        a = arena or ar
        n = int(np.prod(shape)) * DT_SIZE[dtype]
        return Buf(P, name, shape, dtype, a.take(n))

    def _v(x):
        return x.ap if isinstance(x, View) else x

    def ACT(out, in_, func, bias=None, scale=None, accum=None, extra_reads=()):
        reads = [in_] + list(extra_reads)
        writes = [out]
        kw = {}
        if bias is not None:
            kw["bias"] = _v(bias)
            if isinstance(bias, View):
                reads.append(bias)
        if scale is not None:
            kw["scale"] = _v(scale)
            if isinstance(scale, View):
                reads.append(scale)
        if accum is not None:
            kw["accum_out"] = accum.ap
            writes.append(accum)
        P.add("act", lambda e: e.activation(out.ap, in_.ap, func, **kw), reads=reads, writes=writes)

    def TS(eng, out, in0, s1, s2, op0, op1=None):
        reads = [in0] + [s for s in (s1, s2) if isinstance(s, View)]
        if op1 is None:
            P.add(eng, lambda e: e.tensor_scalar(out.ap, in0.ap, _v(s1), None, op0), reads=reads, writes=[out])
        else:
            P.add(eng, lambda e: e.tensor_scalar(out.ap, in0.ap, _v(s1), _v(s2), op0, op1), reads=reads, writes=[out])

    def TT(eng, out, in0, in1, op):
        P.add(eng, lambda e: e.tensor_tensor(out.ap, in0.ap, in1.ap, op), reads=[in0, in1], writes=[out])

    def STT(out, in0, sc, in1, op0, op1):
        reads = [in0, in1] + ([sc] if isinstance(sc, View) else [])
        P.add("dve", lambda e: e.scalar_tensor_tensor(out.ap, in0.ap, _v(sc), in1.ap, op0, op1), reads=reads, writes=[out])

    def RECIP(out, in_):
        P.add("dve", lambda e: e.reciprocal(out.ap, in_.ap), reads=[in_], writes=[out])

    def COPY(eng, out, in_):
        if eng == "act":
            P.add("act", lambda e: e.activation(out.ap, in_.ap, AF.Copy), reads=[in_], writes=[out])
        else:
            P.add(eng, lambda e: e.tensor_copy(out.ap, in_.ap), reads=[in_], writes=[out])

    def MEMSET(eng, out, val):
        P.add(eng, lambda e: e.memset(out.ap, val), writes=[out])

    def MMG(out, items, extra_reads=()):
        reads = list(extra_reads)
        for (l, r) in items:
            reads.append(l)
            reads.append(r)
        n = len(items)

        def fn(e):
            ins = None
            for k, (l, r) in enumerate(items):
                ins = e.matmul(out.ap, l.ap, r.ap, start=(k == 0), stop=(k == n - 1))
            return ins
        P.add("pe", fn, reads=reads, writes=[out])

    def MM(out, l, r, start=True, stop=True):
        P.add("pe", lambda e: e.matmul(out.ap, l.ap, r.ap, start=start, stop=stop), reads=[l, r], writes=[out])

    def DMA(eng, out, in_, sem, reads=(), writes=()):
        P.add(eng, lambda e: e.dma_start(out=_v(out), in_=_v(in_)), reads=list(reads), writes=list(writes), dma_sem=sem)

    sem_id = [0]

    def newsem(prefix="s"):
        sem_id[0] += 1
        return P.new_sem("%s%d" % (prefix, sem_id[0]))

    hT = sb("hT", [16, S], BF16)
    A_off = hT.offset
    Vt = sb("Vt", [16, 512], BF16)
    B_off = Vt.offset
    cosT = sb("cosT", [S], F32)
    sinT = sb("sinT", [S], F32)
    C_off = cosT.offset
    wbuf = [sb("wbuf%d" % i, [16, 128], BF16) for i in range(3)]
    W_off = wbuf[0].offset
    cbf = sb("cbf", [384 + 1024], BF16)
    ident = cbf[:, 0:128]
    perm = cbf[:, 128:256]
    onesb = cbf[:, 256:384]
    maskA = cbf[:, 384:896]
    maskB = cbf[:, 896:1408]
    ones32 = sb("ones32", [128], F32)
    vecs = sb("vecs", [40], F32)
    lamb = sb("lamb", [256], F32)
    sm = sb("sm", [64], F32)
    ssA = sb("ssA", [16], F32)
    sdA = sb("sdA", [16], F32)
    rsA = sb("rsA", [16], F32)
    sp8 = sb("sp8", [4], F32)
    mhalf16 = sb("mhalf16", [16], F32)
    sp16 = sb("sp16", [4], F32)
    wab = sb("wab", [512], BF16)
    wxb = sb("wxb", [512], BF16)
    eps_t = sm[:, 0:1]
    one_t = sm[:, 1:2]
    nlam = sm[:, 2:3]
    gsub = sm[:, 3:4]
    dyn_lo = ar.off
    dyn_hi = SB_BYTES - 256

    accb = [P.psum("acc0", 0), P.psum("acc1", 1)]
    rps = P.psum("rps", 2)
    mps = P.psum("mps", 3)
    Sps = [P.psum("S0", 4), P.psum("S1", 5), rps]
    Ops = P.psum("Ops", 6)
    Zps = P.psum("Zps", 7)

    def ps_bf16_view(buf):
        ap = buf.t[:, 0:512].bitcast(BF16).rearrange("p (c k) -> p c k", c=8)
        return View(ap, "ps", 0, 4, buf.offset, buf.offset + 2048)

    s_c = newsem("c")
    DMA("sp", cbf[:], cbf_d[:, :], s_c, writes=[cbf[:]])
    s_c2 = newsem("c")
    DMA("sp", vecs[:], vec_d[:, :], s_c2, writes=[vecs[:]])
    s_c3 = newsem("c")
    DMA("sp", lamb[:], lam_d[0:1, :].partition_broadcast(128), s_c3, writes=[lamb[:]])
    s_c4 = newsem("c")
    DMA("pool", wab[:], wa_d[:, :], s_c4, writes=[wab[:]])
    s_c5 = newsem("c")
    DMA("pool", wxb[:], wx_d[:, :], s_c5, writes=[wxb[:]])
    MEMSET("dve", ones32[:], 1.0)
    MEMSET("dve", eps_t, EPS)
    MEMSET("dve", one_t, 1.0)
    lt = sb("lt", [2, 64], F32)
    TT("dve", lt[:, 0, :], lamb[:, 0:64], lamb[:, 64:128], ALU.mult)
    TT("dve", lt[:, 1, :], lamb[:, 128:192], lamb[:, 192:256], ALU.mult)
    P.add("dve", lambda e: e.tensor_reduce(sm[:, 4:6].ap, lt[:].ap, AX.X, ALU.add), reads=[lt[:]], writes=[sm[:, 4:6]])
    ACT(sm[:, 6:8], sm[:, 4:6], AF.Exp)
    TT("dve", sm[:, 8:9], sm[:, 7:8], sm[:, 6:7], ALU.subtract)
    TS("dve", nlam, sm[:, 8:9], -LAM_INIT, None, ALU.add)
    TS("dve", gsub, vecs[:, 2:3], 1.0 - LAM_INIT, None, ALU.mult)
    ACT(sm[:, 12:16], vecs[:, 32:36], AF.Exp, scale=-1.0)
    ACT(sm[:, 16:20], sm[:, 12:16], AF.Ln, bias=one_t, scale=1.0)
    TS("dve", sp8[:], sm[:, 16:20], -8.0, None, ALU.mult)
    TS("dve", sp16[:], sm[:, 16:20], -16.0, None, ALU.mult)

    if STOP == 0:
        P.final_wait('sp', [k for k in P.semcount if k not in P.eng_sem.values()]); P.final_wait('pool', list(P.eng_sem.values())); P.emit(); P.close(); return nc
    dyn = Arena(dyn_lo, dyn_hi)
    wvb = sb("wvb", [16, 512], BF16, dyn)
    s_wv = newsem("w")

    def load_wv():
        for q4 in range(4):
            DMA("pool", wvb[:, q4 * 4:(q4 + 1) * 4, :], wv_d[:, q4 * 2048:(q4 + 1) * 2048].rearrange("p (c k) -> p c k", k=512), s_wv,
                writes=[wvb[:, q4 * 4:(q4 + 1) * 4, :]])
    xst = [sb("xst%d" % i, [D], F32, dyn) for i in range(4)]
    junk = sb("junk", [D], BF16, dyn)
    xn = [sb("xn%d" % i, [D], BF16, dyn) for i in range(3)]
    gbc = sb("gbc", [D], F32, dyn)
    mhalf = sm[:, 10:11]
    MEMSET("pool", mhalf, -0.5)
    MEMSET("pool", mhalf16[:], -0.5)
    s_g = newsem("c")
    DMA("sp", gbc[:], ng_d[0:1, :].partition_broadcast(128), s_g, writes=[gbc[:]])
    s_x = [newsem("x") for _ in range(4)]

    def stage1(t):
        xt = xst[t % 4]
        DMA("sp", xt[:], x_d[t * 128:(t + 1) * 128, :], s_x[t % 4], writes=[xt[:]])
        ACT(junk[:], xt[:], AF.Square, accum=ssA[:, t:t + 1])
        TS("pool", sdA[:, t:t + 1], ssA[:, t:t + 1], 1.0 / D, EPS, ALU.mult, ALU.add)
        TT("pool", rsA[:, t:t + 1], sdA[:, t:t + 1], mhalf, ALU.pow)
        STT(xn[t % 3][:], xt[:], rsA[:, t:t + 1], gbc[:], ALU.mult, ALU.mult)

    def stage2(t):
        for half in range(2):
            pb = accb[half]
            pv = ps_bf16_view(pb)
            xnb = xn[t % 3]

            def fn(e, pb=pb, xnb=xnb, half=half):
                ins = None
                bf = pb.t[:, 0:512].bitcast(BF16)
                for c in range(8):
                    cc = half * 8 + c
                    ins = e.transpose(bf[:, c * 128:(c + 1) * 128], xnb.t[:, cc * 128:(cc + 1) * 128], ident.ap)
                return ins
            P.add("pe", fn, reads=[xnb[:], ident], writes=[pv])
            dst = hT[:, half * 8:(half + 1) * 8, t * 128:(t + 1) * 128]
            if half == 0:
                P.add("act", lambda e, pv=pv, dst=dst: e.activation(dst.ap, pv.ap, AF.Copy), reads=[pv], writes=[dst])
            else:
                P.add("dve", lambda e, pv=pv, dst=dst: e.tensor_copy(dst.ap, pv.ap), reads=[pv], writes=[dst])

    vbanks = [rps, mps]

    def stage3(t):
        acc = vbanks[t % 2]
        MMG(acc[:], [(hT[:, c, t * 128:(t + 1) * 128], wvb[:, c, :]) for c in range(16)])
        COPY("act", Vt[:, t, :], acc[:])

    for t in range(16 + 2):
        if t < 16:
            stage1(t)
        if t == 4:
            load_wv()
        if 0 <= t - 2 < 16:
            stage2(t - 2)
    dyn = Arena(dyn_lo + 16384, dyn_hi)
    posi = sb("posi", [S], I32, dyn)
    tt_ = sb("ttab", [S], F32, dyn)
    ti_ = sb("titab", [S], I32, dyn)
    tf_ = sb("tftab", [S], F32, dyn)
    tf2_ = sb("tf2tab", [S], F32, dyn)
    s_p = newsem("c")
    DMA("sp", posi[:], pos_d[0:1, :].partition_broadcast(128), s_p, writes=[posi[:]])
    COPY("dve", tf_[:], posi[:])
    TS("dve", tt_[:], tf_[:], vecs[:, 0:1], None, ALU.mult)
    COPY("dve", ti_[:], tt_[:])
    COPY("dve", tf_[:], ti_[:])
    TT("dve", tf_[:], tt_[:], tf_[:], ALU.subtract)
    TS("dve", tt_[:], tt_[:], 0.25, None, ALU.add)
    COPY("dve", ti_[:], tt_[:])
    COPY("dve", tf2_[:], ti_[:])
    TT("dve", tf2_[:], tt_[:], tf2_[:], ALU.subtract)
    for t in range(16):
        stage3(t)
        if t == 6:
            ACT(sinT[:], tf_[:], AF.Sin, scale=TWO_PI_SCALE)
        if t == 11:
            ACT(cosT[:], tf2_[:], AF.Sin, scale=TWO_PI_SCALE)
    TS("dve", sinT[:], sinT[:], vecs[:, 1:2], None, ALU.mult)

    if STOP == 2:
        P.final_wait('sp', [k for k in P.semcount if k not in P.eng_sem.values()]); P.final_wait('pool', list(P.eng_sem.values())); P.emit(); P.close(); return nc
    s_w = [newsem("w") for _ in range(3)]
    wstate = {"next": 0}

    def w_prefetch(upto):
        while wstate["next"] <= min(upto, 19):
            ct = wstate["next"]
            DMA("pool", wbuf[ct % 3][:], wct_d[ct, :, :].rearrange("p (c k) -> p c k", k=128), s_w[ct % 3], writes=[wbuf[ct % 3][:]])
            wstate["next"] += 1

    gcount = [0]

    acc_banks = [accb[0], accb[1]]

    def inproj_group(ct, tb):
        acc = acc_banks[gcount[0] % len(acc_banks)]
        gcount[0] += 1
        wb = wbuf[ct % 3]
        MMG(acc[:], [(wb[:, c, :], hT[:, c, tb * 512:(tb + 1) * 512]) for c in range(16)])
        return acc

    dyn = Arena(dyn_lo, dyn_hi)
    hsets = []
    for i in range(2):
        qz = sb("qz%d" % i, [8, 2, 256], BF16, dyn)
        kT = sb("kT%d" % i, [S], BF16, dyn)
        gs = sb("gs%d" % i, [S], BF16, dyn)
        mixh = sb("mixh%d" % i, [S], BF16, dyn)
        hsets.append((qz, kT, gs, mixh))
    ET = [sb("ET%d" % i, [512], BF16, dyn) for i in range(3)]
    rsb = sb("rsb", [512], F32, dyn)
    tbuf = sb("tbuf", [512], F32, dyn)
    obufs = [sb("obuf%d" % i, [256], F32, dyn) for i in range(2)]
    osqs = [sb("osq%d" % i, [256], F32, dyn) for i in range(2)]
    sdbs = [sb("sdb%d" % i, [256], F32, dyn) for i in range(2)]
    rstds = [sb("rstd%d" % i, [256], F32, dyn) for i in range(2)]
    onbs = [sb("onb%d" % i, [256], F32, dyn) for i in range(2)]
    head_end = dyn.off
    tail = Arena(dyn_hi - 10 * 1024, dyn_hi)
    qbf = [sb("qbf%d" % i, [512], BF16, tail) for i in range(2)]
    t1b = [sb("t1b%d" % i, [2, 256], F32, tail) for i in range(2)]
    t2b = [sb("t2b%d" % i, [2, 256], F32, tail) for i in range(2)]
    TAIL_LO = dyn_hi - 10 * 1024
    WO_END = head_end + 4096
    rdyn = Arena(dyn_lo, TAIL_LO)
    xpads = [sb("xpad%d" % i, [S + 3], F32, rdyn) for i in range(2)]
    cvbs = [sb("cvb%d" % i, [S], BF16, rdyn) for i in range(2)]
    gsrs = [sb("gsr%d" % i, [S], BF16, rdyn) for i in range(3)]
    cvs = [sb("cv%d" % i, [S], F32, rdyn) for i in range(2)]
    rgf = sb("rgf", [S], F32, rdyn)
    igf = sb("igf", [S], F32, rdyn)
    afull = sb("afull", [S], F32, rdyn)
    mixr = sb("mixr", [S], BF16, rdyn)


    rope_cnt = [0]

    rope_pending = []

    def flush_rope():
        while rope_pending:
            rope_pending.pop(0)()

    def evac_rope(acc, tb, kind, hs):
        i = rope_cnt[0] % 2
        rope_cnt[0] += 1
        tok = slice(tb * 512, (tb + 1) * 512)
        COPY("act", qbf[i][:], acc[:])
        t1v = t1b[i][:]
        t2v = t2b[i][:]
        P.add("dve", lambda e, a=acc, o=t1b[i]: e.tensor_tensor(o.t[:, :, :].rearrange("p a b -> p (a b)"), a.t[:, 0:512], cosT.t[:, tok], ALU.mult),
              reads=[acc[:], cosT[:, tok]], writes=[t1v])

        def part2():
            MM(mps[:], perm, qbf[i][:])
            P.add("dve", lambda e, o=t2b[i]: e.tensor_tensor(o.t[:, :, :].rearrange("p a b -> p (a b)"), mps.t[:, 0:512], sinT.t[:, tok], ALU.mult),
                  reads=[mps[:], sinT[:, tok]], writes=[t2v])
            if kind == "q":
                qz = hs[0]
                for m in range(2):
                    ps = slice(64 * m, 64 * (m + 1))
                    TT("pool", qz[ps, 2 * tb:2 * tb + 2, m, :], t1b[i][ps, :, :], t2b[i][ps, :, :], ALU.add)
            else:
                kT = hs[1]
                P.add("pool", lambda e, o=kT, a=t1b[i], b=t2b[i]: e.tensor_tensor(
                    o.t[:, tok], a.t[:, :, :].rearrange("p a b -> p (a b)"), b.t[:, :, :].rearrange("p a b -> p (a b)"), ALU.add),
                    reads=[t1v, t2v], writes=[kT[:, tok]])
        rope_pending.append(part2)

    def head_inproj_tasks(h, hs):
        tasks = []
        base = 8 + 3 * h
        for j, kind in enumerate(("q", "k", "g")):
            ct = base + j
            for tb in range(4):
                def task(ct=ct, tb=tb, kind=kind):
                    if tb == 0:
                        w_prefetch(ct + 2)
                    acc = inproj_group(ct, tb)
                    flush_rope()

                    def evac(acc=acc):
                        if kind == "g":
                            ACT(hs[2][:, tb * 512:(tb + 1) * 512], acc[:], AF.Silu)
                        else:
                            evac_rope(acc, tb, kind, hs)
                    return evac
                tasks.append(task)
        return tasks

    s_mr = newsem("m")
    w_prefetch(1)
    h0_tasks = None
    for i in range(2):
        MEMSET("dve", xpads[i][:, 0:3], 0.0)
    gps = [rps, mps]

    def rec_inproj(n):
        xpad, gsr = xpads[n % 2], gsrs[n % 3]
        ctx_, ctg = 2 * n, 2 * n + 1
        w_prefetch(ctx_ + 2)
        for tb in range(4):
            acc = inproj_group(ctx_, tb)
            COPY("act", xpad[:, 3 + tb * 512:3 + (tb + 1) * 512], acc[:])
        w_prefetch(ctg + 2)
        for tb in range(4):
            acc = inproj_group(ctg, tb)
            ACT(gsr[:, tb * 512:(tb + 1) * 512], acc[:], AF.Silu)

    def rec_conv(n):
        xpad, cv, cvb = xpads[n % 2], cvs[n % 2], cvbs[n % 2]
        cw = lambda j: vecs[:, 4 + n * 4 + j:5 + n * 4 + j]
        TS("dve", cv[:], xpad[:, 0:S], cw(0), vecs[:, 20 + n:21 + n], ALU.mult, ALU.add)
        for j in range(1, 4):
            STT(cv[:], xpad[:, j:j + S], cw(j), cv[:], ALU.mult, ALU.add)
        COPY("dve", cvb[:], cv[:])

    def rec_gates(n):
        cvb = cvbs[n % 2]
        for tb in range(4):
            tok = slice(tb * 512, (tb + 1) * 512)
            MM(gps[0][:], wab[:, n * 128:(n + 1) * 128], cvb[:, tok])
            ACT(rgf[:, tok], gps[0][:], AF.Sigmoid, bias=vecs[:, 24 + n:25 + n], scale=1.0)
            MM(gps[1][:], wxb[:, n * 128:(n + 1) * 128], cvb[:, tok])
            ACT(igf[:, tok], gps[1][:], AF.Sigmoid, bias=vecs[:, 28 + n:29 + n], scale=1.0)

    def rec_post(n, part=None):
        cv, gsr = cvs[n % 2], gsrs[n % 3]
        if part in (None, "a"):
            TT("dve", igf[:], igf[:], cv[:], ALU.mult)
            ACT(afull[:], rgf[:], AF.Exp, scale=sp8[:, n:n + 1])
            ACT(rgf[:], rgf[:], AF.Exp, scale=sp16[:, n:n + 1])
            ACT(rgf[:], rgf[:], AF.Ln, bias=one_t, scale=-1.0)
            ACT(rgf[:], rgf[:], AF.Exp, scale=0.5)
        if part == "a":
            return
        TT("dve", igf[:], igf[:], rgf[:], ALU.mult)
        P.add("dve", lambda e: e.tensor_tensor_scan(rgf[:].ap, afull[:].ap, igf[:].ap, 0.0, ALU.mult, ALU.add),
              reads=[afull[:], igf[:]], writes=[rgf[:]])
        TT("dve", mixr[:], rgf[:], gsr[:], ALU.mult)
        DMA("sp", mixR_in[n * 128:(n + 1) * 128, :], mixr[:], s_mr, reads=[mixr[:]], writes=[Dram("mixR_in")])

    rec_inproj(0)
    rec_conv(0)
    rec_inproj(1)
    for n in range(NRB):
        rec_gates(n)
        if n + 1 < NRB:
            rec_conv(n + 1)
        if n + 2 < NRB:
            rec_post(n)
            rec_inproj(n + 2)
            continue
        if n == NRB - 2:
            rec_post(n)
            h0_tasks = head_inproj_tasks(0, hsets[0])
            MEMSET("pool", hsets[0][0][:], 0.0)
            acc_banks.extend([Sps[0], Sps[1], Ops, Zps])
            for tk in h0_tasks[0:4]:
                tk()()
            continue
        if n == NRB - 1:
            rec_post(n, "a")
            for tk in h0_tasks[4:12]:
                tk()()
            flush_rope()
            rec_post(n, "b")
            del acc_banks[2:]
            gcount[0] = 0
            continue
        rec_post(n)

    if STOP == 3:
        P.final_wait('sp', [k for k in P.semcount if k not in P.eng_sem.values()]); P.final_wait('pool', list(P.eng_sem.values())); P.emit(); P.close(); return nc
    s_cc1 = newsem("cc")

    def cc_op(kind_in, kind_out, tin, tout, sem):
        op = P.add("pool", lambda e: e.collective_compute("AllGather", ALU.bypass, replica_groups=PAIRS,
                                                          ins=[tin.ap().opt()], outs=[tout.ap().opt()]),
                   reads=[Dram(kind_in)], writes=[Dram(kind_out)], dma_sem=sem, cc=True)
    cc_op("mixR_in", "mixR_out", mixR_in, mixR_out, s_cc1)

    if STOP == 4:
        P.final_wait('sp', [k for k in P.semcount if k not in P.eng_sem.values()]); P.final_wait('pool', list(P.eng_sem.values())); P.emit(); P.close(); return nc
    MEMSET("dve", hsets[1][0][:], 0.0)
    if STOP == 401:
        P.final_wait('sp', [k for k in P.semcount if k not in P.eng_sem.values()]); P.final_wait('pool', list(P.eng_sem.values())); P.emit(); P.close(); return nc
    if STOP == 41:
        P.final_wait('sp', [k for k in P.semcount if k not in P.eng_sem.values()]); P.final_wait('pool', list(P.eng_sem.values())); P.emit(); P.close(); return nc
    epi_cnt = [0]
    s_ma = newsem("m")
    s_ma2 = newsem("m")
    s_cc2a = newsem("cc")
    pair_cnt = [0]
    srcs = [(mixR_out, "mixR_out", 0, 4, 0), (mixR_out, "mixR_out", 1, 4, 4), (mixA1_out, "mixA1_out", 0, 3, 8),
            (mixA1_out, "mixA1_out", 1, 3, 11), (mixA2_out, "mixA2_out", 0, 1, 14), (mixA2_out, "mixA2_out", 1, 1, 15)]

    def chunk_src(c):
        for (src, key, slot, nch, c0) in srcs:
            if c0 <= c < c0 + nch:
                return src, key, (slot * nch + (c - c0)) * 128
    mixc = [None] * 16
    early_offs = [WO_END + i * 4096 for i in range(4)] + [TAIL_LO + i * 4096 for i in range(2)] + [dyn_lo + i * 4096 for i in range(5)]
    assert WO_END + 4 * 4096 <= TAIL_LO
    for c in range(11):
        mixc[c] = Buf(P, "mixc%d" % c, [S], BF16, early_offs[c])
    s_mxe = newsem("mx")

    def load_mix_early():
        for c in range(11):
            src, key, r0 = chunk_src(c)
            DMA("sp", mixc[c][:], src.ap()[r0:r0 + 128, :], s_mxe, reads=[Dram(key)], writes=[mixc[c][:]])

    def load_wo():
        wo_bufs = []
        carena = Arena(C_off, C_off + 16384)
        for c in range(8):
            wo_bufs.append(sb("wo%d" % c, [1024], BF16, carena))
        warena = Arena(W_off, W_off + 12288)
        for c in range(8, 14):
            wo_bufs.append(sb("wo%d" % c, [1024], BF16, warena))
        odyn = Arena(head_end, head_end + 4096)
        for c in range(14, 16):
            wo_bufs.append(sb("wo%d" % c, [1024], BF16, odyn))
        s_wo = newsem("w")
        for c in range(16):
            DMA("pool", wo_bufs[c][:], wo_d[c, :, :], s_wo, writes=[wo_bufs[c][:]])
        return wo_bufs

    wo_box = [None]
    deferred = []
    evac_pending = []
    nxt_box = [[]]
    gpairs = [(h, j, kt) for h in range(NH) for j in range(8) for kt in range(2 * (j + 1))]
    ngp = len(gpairs)

    def issue_S(gi):
        h, j, kt = gpairs[gi]
        qz, kT, gs, mixh = hsets[h % 2]
        Sb = Sps[gi % 3]
        qv = qz[:, j, :, :]
        ktv = kT[:, kt * 128:(kt + 1) * 128]
        if kt == 2 * j + 1:
            def fnb(e, Sb=Sb, qz=qz, kT=kT, j=j, kt=kt):
                ins = None
                for m in range(2):
                    o2 = Sb.t[:, m * 256 + 128:m * 256 + 256]
                    e.matmul(o2, kT.t[:, kt * 128:(kt + 1) * 128], qz.t[:, j, m, 128:256], start=True, stop=False)
                    ins = e.matmul(o2, ident.ap, cbf.t[:, 896 + m * 256 + 128:896 + m * 256 + 256], start=False, stop=True)
                return ins
            P.add("pe", fnb, reads=[ktv, qv, ident, maskB], writes=[Sb[:]])
            return
        diag = (kt == 2 * j)

        def fn(e, Sb=Sb, qz=qz, kT=kT, j=j, kt=kt, diag=diag):
            ins = e.matmul(Sb.t[:, 0:512], kT.t[:, kt * 128:(kt + 1) * 128],
                           qz.t[:, j, :, :].rearrange("p a b -> p (a b)"), start=True, stop=(not diag))
            if diag:
                ins = e.matmul(Sb.t[:, 0:512], ident.ap, maskA.ap, start=False, stop=True)
            return ins
        P.add("pe", fn, reads=[ktv, qv] + ([ident, maskA] if diag else []), writes=[Sb[:]])

    followups = []
    followups_a = []

    def flush_followups():
        while followups:
            followups.pop(0)()

    def pop_tasks(j):
        n = (1, 1, 1, 1, 1, 1, 2, 4)[j]
        nxt = nxt_box[0]
        for _ in range(n):
            if nxt:
                while len(evac_pending) >= 2:
                    evac_pending.pop(0)()
                evac_pending.append(nxt.pop(0)())

    def finish_head(h):
        mixh = hsets[h % 2][3]
        if h < 3:
            DMA("sp", mixA1_in[h * 128:(h + 1) * 128, :], mixh[:], s_ma, reads=[mixh[:]], writes=[Dram("mixA1_in")])
        else:
            DMA("sp", mixA2_in[0:128, :], mixh[:], s_ma2, reads=[mixh[:]], writes=[Dram("mixA2_in")])
        if h == 2:
            cc_op("mixA1_in", "mixA1_out", mixA1_in, mixA1_out, s_cc2a)
            load_mix_early()

    def issue_rest(gi):
        h, j, kt = gpairs[gi]
        qz, kT, gs, mixh = hsets[h % 2]
        nkt = 2 * (j + 1)
        Sb = Sps[gi % 3]
        E = ET[gi % 3]
        if kt == 2 * j + 1:
            def v3(t):
                return t[:, 0:512].rearrange("p (m x) -> p m x", m=2)[:, :, 128:256]
            P.add("act", lambda e, Sb=Sb, E=E: e.activation(v3(E.t), v3(Sb.t), AF.Exp, scale=0.125),
                  reads=[Sb[:]], writes=[E[:]])
            Vv = Vt[:, kt, h * 128:(h + 1) * 128]

            def c2(t, m):
                return t[:, m * 256 + 128:m * 256 + 256]

            def fpv(e, E=E, Vv=Vv):
                e.matmul(c2(Ops.t, 0), Vv.ap, c2(E.t, 0), start=False, stop=False)
                return e.matmul(c2(Ops.t, 1), Vv.ap, c2(E.t, 1), start=False, stop=True)

            def fz(e, E=E):
                e.matmul(c2(Zps.t, 0), onesb.ap, c2(E.t, 0), start=False, stop=False)
                return e.matmul(c2(Zps.t, 1), onesb.ap, c2(E.t, 1), start=False, stop=True)
            P.add("pe", fpv, reads=[Vv, E[:]], writes=[Ops[:]])
            P.add("pe", fz, reads=[onesb, E[:]], writes=[Zps[:]])
        else:
            ACT(E[:], Sb[:], AF.Exp, scale=0.125)
            MM(Ops[:], Vt[:, kt, h * 128:(h + 1) * 128], E[:], start=(kt == 0), stop=(kt == nkt - 1))
            MM(Zps[:], onesb, E[:], start=(kt == 0), stop=(kt == nkt - 1))
        if kt == nkt - 1:
            ACT(rsb[:], Zps[:], AF.Ln)
            COPY("dve", tbuf[:], Ops[:])
        while followups_a:
            followups_a.pop(0)()
        flush_followups()
        while evac_pending:
            evac_pending.pop(0)()
        flush_rope()
        for d_ in list(deferred):
            d_[0] -= 1
            if d_[0] <= 0 and deferred and deferred[0] is d_:
                r_ = deferred.pop(0)[1]()
                if r_ is not None:
                    followups.append(r_)
        if kt != nkt - 1:
            return
        qs = slice(j * 256, (j + 1) * 256)
        eb = epi_cnt[0] % 2
        epi_cnt[0] += 1
        while len(deferred) > 0 and deferred[0][2] <= epi_cnt[0] - 2:
            r_ = deferred.pop(0)[1]()
            if r_ is not None:
                followups.append(r_)
        if followups and epi_cnt[0] >= 2:
            flush_followups()
        obuf, osq, sdb, rstd, onb = obufs[eb], osqs[eb], sdbs[eb], rstds[eb], onbs[eb]
        pop_tasks(j)
        def stage_b(qs=qs, gs=gs, mixh=mixh, obuf=obuf, osq=osq, sdb=sdb, rstd=rstd, onb=onb):
            MM(mps[:, 0:256], ones32[:], osq[:])

            def stage_c():
                ACT(sdb[:], mps[:, 0:256], AF.Ln, bias=eps_t, scale=1.0 / 128.0)
                ACT(rstd[:], sdb[:], AF.Exp, scale=-0.5)
                STT(onb[:], obuf[:], gsub, rstd[:], ALU.mult, ALU.mult)
                TT("pool", mixh[:, qs], onb[:], gs[:, qs], ALU.mult)
            return stage_c

        def stage_a(j=j, h=h, obuf=obuf, osq=osq, stage_b=stage_b, cnt=epi_cnt[0]):
            ACT(rsb[:], rsb[:], AF.Exp, scale=-1.0)
            TT("dve", tbuf[:], tbuf[:], rsb[:], ALU.mult)
            STT(obuf[:], tbuf[:, 256:512], nlam, tbuf[:, 0:256], ALU.mult, ALU.add)
            TT("pool", osq[:], obuf[:], obuf[:], ALU.mult)
            deferred.append([10, stage_b, cnt])
            if j == 7:
                deferred.append([11, lambda h=h: (flush_followups(), finish_head(h), None)[2], cnt])
        followups_a.append(stage_a)

    issue_S(0)
    issue_S(1)
    for gi in range(ngp):
        h, j, kt = gpairs[gi]
        if j == 0 and kt == 0:
            assert not nxt_box[0]
            nxt_box[0] = head_inproj_tasks(h + 1, hsets[(h + 1) % 2]) if h + 1 < NH else []
            if h == NH - 1:
                wo_box[0] = load_wo()
        if gi + 2 < ngp:
            issue_S(gi + 2)
        issue_rest(gi)
    while evac_pending:
        evac_pending.pop(0)()
    flush_rope()
    while followups_a:
        followups_a.pop(0)()
    flush_followups()
    while deferred:
        r_ = deferred.pop(0)[1]()
        if r_ is not None:
            r_()
    wo_bufs = wo_box[0]

    if STOP == 5:
        P.final_wait('sp', [k for k in P.semcount if k not in P.eng_sem.values()]); P.final_wait('pool', list(P.eng_sem.values())); P.emit(); P.close(); return nc
    s_cc2 = newsem("cc")
    cc_op("mixA2_in", "mixA2_out", mixA2_in, mixA2_out, s_cc2)

    odyn = Arena(dyn_lo + 20 * 1024, head_end)
    xrt = [sb("xrt%d" % i, [1024], F32, odyn) for i in range(2)]
    ostg = [sb("ostg%d" % i, [1024], F32, odyn) for i in range(3)]
    fgb = sb("fgb", [1024], F32, odyn)
    junk2 = sb("junk2", [1024], BF16, odyn)
    ssqh = [sb("ssq%d" % i, [8], F32, odyn) for i in range(4)]
    ssgh = [sb("ssg%d" % i, [2, 8], F32, odyn) for i in range(4)]
    sdoh = [sb("sdo%d" % i, [8], F32, odyn) for i in range(4)]
    rsoh = [sb("rso%d" % i, [8], F32, odyn) for i in range(4)]
    GB = [0, 8, 14, 16]
    grp_of = {}
    NGRP = len(GB) - 1
    for g_ in range(NGRP):
        for T_ in range(GB[g_], GB[g_ + 1]):
            grp_of[T_] = (g_, T_ - GB[g_])
    for g_ in range(NGRP):
        MEMSET("pool", ssqh[g_][:], 1.0)
    ybuf = Buf(P, "ybuf", [16, 1024], F32, A_off)
    for c in range(11, 15):
        mixc[c] = Buf(P, "mixc%d" % c, [S], BF16, B_off + (c - 11) * 4096)
    mixc[15] = sb("mixc15", [S], BF16, odyn)
    s_fg = newsem("c")
    DMA("sp", fgb[:], fg_d[0:1, :].partition_broadcast(128), s_fg, writes=[fgb[:]])
    s_xr = [newsem("xr"), newsem("xr")]
    yps = [(accb[0], accb[1]), (rps, mps)]
    s_mx = [newsem("mx"), newsem("mx")]

    s_mxa = [newsem("mx"), newsem("mx")]

    def load_mix(half, chunks, sems):
        tk = slice(half * 1024, (half + 1) * 1024)
        for c in chunks:
            src, key, r0 = chunk_src(c)
            DMA("sp", mixc[c][:, tk], src.ap()[r0:r0 + 128, half * 1024:(half + 1) * 1024], sems[half],
                reads=[Dram(key)], writes=[mixc[c][:, tk]])

    s_out = [newsem("o") for _ in range(3)]

    ex_sems = {}
    ex_done = set()

    def exchange(half):
        s1, s2, s3 = newsem("c"), newsem("cc"), newsem("c")
        ex_sems[half] = s3
        DMA("act", ss_in_h[half][:, :], ssqh[half][:], s1, reads=[ssqh[half][:]], writes=[Dram("ss_in%d" % half)])
        cc_op("ss_in%d" % half, "ss_out%d" % half, ss_in_h[half], ss_out_h[half], s2)

    def exchange_finish(half):
        if half in ex_done:
            return
        ex_done.add(half)
        DMA("sp" if half == NGRP - 1 else "pool", ssgh[half][:], ss_out_h[half].ap().rearrange("(s p) t -> p s t", p=128),
            ex_sems[half], reads=[Dram("ss_out%d" % half)], writes=[ssgh[half][:]])
        TT("pool", sdoh[half][:], ssgh[half][:, 0, :], ssgh[half][:, 1, :], ALU.add)
        TS("pool", sdoh[half][:], sdoh[half][:], 1.0 / D, EPS, ALU.mult, ALU.add)
        TT("pool", rsoh[half][:], sdoh[half][:], mhalf16[:, 0:8], ALU.pow)

    def finalize_tile(T):
        og = ostg[T % 3]
        g_, k_ = grp_of[T]
        exchange_finish(g_)
        STT(og[:], ybuf[:, T, :], rsoh[g_][:, k_:k_ + 1], fgb[:], ALU.mult, ALU.mult)
        DMA("sp", y_d[T * 128:(T + 1) * 128, :], og[:], s_out[T % 3], reads=[og[:]])

    def load_xres(T):
        DMA("sp", xrt[T % 2][:], xres_d[T * 128:(T + 1) * 128, :], s_xr[T % 2], writes=[xrt[T % 2][:]])

    load_mix(0, [11, 12, 13], s_mx)
    load_xres(0)
    load_xres(1)
    load_mix(1, [11, 12, 13], s_mx)
    load_mix(0, [14, 15], s_mxa)
    load_mix(1, [14, 15], s_mxa)

    def partial_group(yp, T, nb, chunks, first, last):
        items = [(mixc[c][:, T * 128:(T + 1) * 128], wo_bufs[c][:, nb * 512:(nb + 1) * 512]) for c in chunks]
        reads = [v for it in items for v in it]

        def fn(e):
            ins = None
            for k, (l, r) in enumerate(items):
                ins = e.matmul(yp.t[:, 0:512], l.ap, r.ap, start=(first and k == 0), stop=(last and k == len(items) - 1))
            return ins
        P.add("pe", fn, reads=reads, writes=[yp[:]])

    pre_banks = [accb[0], accb[1], rps, mps, Sps[0], Sps[1], Ops, Zps]
    NPRE = 4
    for T in range(NPRE):
        for nb in range(2):
            partial_group(pre_banks[2 * T + nb], T, nb, list(range(14)), True, False)
    fin_queue = []
    for tb in range(4):
        for tt in range(4):
            T = 4 * tb + tt
            xr_t = xrt[T % 2]
            for nb in range(2):
                if T < NPRE:
                    yp = pre_banks[2 * T + nb]
                    partial_group(yp, T, nb, [14, 15], False, True)
                else:
                    yp = yps[T % 2][nb]
                    MMG(yp[:], [(mixc[c][:, T * 128:(T + 1) * 128], wo_bufs[c][:, nb * 512:(nb + 1) * 512]) for c in range(16)])
                TT("dve", ybuf[:, T, nb * 512:(nb + 1) * 512], yp[:], xr_t[:, nb * 512:(nb + 1) * 512], ALU.add)
            g_, k_ = grp_of[T]
            ACT(junk2[:], ybuf[:, T, :], AF.Square, accum=ssqh[g_][:, k_:k_ + 1])
            if T + 2 < 16:
                load_xres(T + 2)
            for _ in range(2):
                if fin_queue and fin_queue[0][0] <= T:
                    finalize_tile(fin_queue.pop(0)[1])
            if T + 1 == GB[g_ + 1]:
                exchange(g_)
                for T2 in range(GB[g_], GB[g_ + 1]):
                    fin_queue.append((T + 3, T2))
    while fin_queue:
        finalize_tile(fin_queue.pop(0)[1])
    P.final_wait("sp", s_out)
    P.emit()
    P.close()
    return nc


def _host_layout(inputs):
    f32 = np.float32
    x = np.asarray(inputs["x"], f32)
    pos = np.asarray(inputs["positions"], np.int32)
    w_in = np.asarray(inputs["w_in"], f32)[0]
    w_out = np.asarray(inputs["w_out"], f32)[0]
    ng = np.asarray(inputs["norm_gain"], f32)[0]
    fg = np.asarray(inputs["final_gain"], f32)
    lamv = np.stack([np.asarray(inputs[k], f32)[0] for k in ("lambda_q1", "lambda_k1", "lambda_q2", "lambda_k2")], 0)
    subln = np.asarray(inputs["subln_gain"], f32)[0]
    conv_w = np.asarray(inputs["conv_w"], f32)[0]
    conv_b = np.asarray(inputs["conv_b"], f32)[0]
    w_a = np.asarray(inputs["w_a"], f32)[0]
    w_x = np.asarray(inputs["w_x"], f32)[0]
    b_a = np.asarray(inputs["b_a"], f32)[0]
    b_x = np.asarray(inputs["b_x"], f32)[0]
    lru = np.asarray(inputs["lru_lambda"], f32)[0]

    p = np.arange(128)
    d = p % 64
    invf = (10000.0 ** (-(2.0 * (d % 32)) / 64.0) / (2.0 * np.pi)).astype(f32)
    sgn = np.where(d < 32, -1.0, 1.0).astype(f32)
    ident = np.eye(128, dtype=f32)
    partner = (p // 64) * 64 + (d + 32) % 64
    perm = np.zeros((128, 128), f32)
    perm[partner, p] = 1.0
    ones = np.ones((128, 128), f32)
    xq = np.arange(256)[None, :]
    mA = np.where(xq >= p[:, None], 0.0, -30000.0).astype(f32)
    mB = np.where(xq >= (128 + p)[:, None], 0.0, -30000.0).astype(f32)
    cbf = np.concatenate([ident, perm, ones, mA, mA, mB, mB], 1).astype(ml_dtypes.bfloat16)

    maps = []
    for core in range(8):
        b, hh = core // 2, core % 2
        m = {}
        m["x"] = np.ascontiguousarray(x[b])
        m["xres"] = np.ascontiguousarray(x[b][:, hh * 1024:(hh + 1) * 1024])
        m["pos"] = np.ascontiguousarray(pos[b][None, :])
        m["ng"] = np.ascontiguousarray(ng[None, :])
        vcols = 2048 + (4 * hh) * 128 + np.arange(512)
        wv = w_in[:, vcols].reshape(16, 128, 512).transpose(1, 0, 2)
        m["wv"] = np.ascontiguousarray(wv.reshape(128, 16 * 512))
        cts = []
        for n in range(4):
            cts.append(4096 + (4 * hh + n) * 128)
            cts.append(5120 + (4 * hh + n) * 128)
        for h in range(4):
            cts.append(0 + (4 * hh + h) * 128)
            cts.append(1024 + (4 * hh + h) * 128)
            cts.append(3072 + (4 * hh + h) * 128)
        wct = np.stack([w_in[:, c0:c0 + 128].reshape(16, 128, 128).transpose(1, 0, 2).reshape(128, 2048) for c0 in cts], 0)
        m["wct"] = np.ascontiguousarray(wct)
        rows = []
        for s in range(2):
            for n in range(4):
                rows.append(1024 + (4 * s + n) * 128)
        for s in range(2):
            for h in range(3):
                rows.append((4 * s + h) * 128)
        for s in range(2):
            rows.append((4 * s + 3) * 128)
        wo = np.stack([w_out[r0:r0 + 128, hh * 1024:(hh + 1) * 1024] for r0 in rows], 0)
        m["wo"] = np.ascontiguousarray(wo)
        m["lamv"] = np.ascontiguousarray(lamv.reshape(1, 256))
        vecs = np.zeros((128, 40), f32)
        vecs[:, 0] = invf
        vecs[:, 1] = sgn
        vecs[:, 2] = subln
        for n in range(4):
            ch = (4 * hh + n) * 128 + p
            for j in range(4):
                vecs[:, 4 + n * 4 + j] = conv_w[j, ch]
            vecs[:, 20 + n] = conv_b[ch]
            vecs[:, 24 + n] = b_a[ch]
            vecs[:, 28 + n] = b_x[ch]
            vecs[:, 32 + n] = lru[ch]
        m["vecs"] = vecs
        m["fg"] = np.ascontiguousarray(fg[None, hh * 1024:(hh + 1) * 1024])
        m["wa"] = np.ascontiguousarray(w_a[4 * hh:4 * hh + 4].transpose(1, 0, 2).reshape(128, 512))
        m["wx"] = np.ascontiguousarray(w_x[4 * hh:4 * hh + 4].transpose(1, 0, 2).reshape(128, 512))
        m["cbf"] = cbf
        maps.append(m)
    return maps


_NC_CACHE = {}


def kernel(**inputs):
    maps = _host_layout(inputs)
    if "nc" not in _NC_CACHE:
        _NC_CACHE["nc"] = build_program()
    nc = _NC_CACHE["nc"]
    res = run_bass_kernel_spmd(nc, maps, core_ids=list(range(8)))
    out = np.zeros((4, S, D), np.float32)
    for core in range(8):
        b, hh = core // 2, core % 2
        out[b][:, hh * 1024:(hh + 1) * 1024] = np.asarray(res.results[core]["y"], np.float32)
    return out
```
